# Optimizing a Trainium2 kernel written in Bass

```python
import jax, jax.numpy as jnp
from jax import lax
import numpy as np

D_MODEL = 1024
BATCH = 8
SEQ = 2048
DEPTH = 4
DEC_BATCH = 32
DEC_SEQ = 32
PAST_LEN = 1024

CHUNK = 64
WINDOW = 128
WIN_CHUNKS = WINDOW // CHUNK
HEAD_DIM = 64
N_HEADS = D_MODEL // HEAD_DIM
N_KV_HEADS = 4
GROUP = N_HEADS // N_KV_HEADS
Q_DIM = N_HEADS * HEAD_DIM
KV_DIM = N_KV_HEADS * HEAD_DIM
CONV_DIM = D_MODEL
CONV_WIDTH = 3
N_MEM = 256
MEM_HEADS = 4
MEM_HEAD_DIM = 64
MEM_DIM = MEM_HEADS * MEM_HEAD_DIM
D_FF = 4 * D_MODEL
EPS = 1e-6
ATTN_SCALE = HEAD_DIM ** -0.5
MEM_SCALE = MEM_HEAD_DIM ** -0.5
IN_COLS = Q_DIM + 2 * KV_DIM + 3 * CONV_DIM + 2 * D_MODEL
SPLITS = (Q_DIM, Q_DIM + KV_DIM, Q_DIM + 2 * KV_DIM, Q_DIM + 2 * KV_DIM + CONV_DIM,
          Q_DIM + 2 * KV_DIM + 2 * CONV_DIM, Q_DIM + 2 * KV_DIM + 3 * CONV_DIM,
          Q_DIM + 2 * KV_DIM + 3 * CONV_DIM + D_MODEL)

kernel_name = 'hybrid_chunk_stream_swa_sink_shortconv_step'


def rms_norm(x, g):
    xf = x.astype(jnp.float32)
    y = xf * lax.rsqrt(jnp.mean(xf * xf, axis=-1, keepdims=True) + EPS)
    return (y * g.astype(jnp.float32)).astype(x.dtype)


def alibi_slopes():
    return jnp.exp2(-8.0 * jnp.arange(1, N_HEADS + 1, dtype=jnp.float32) / N_HEADS)


def sink_softmax(s, sink):
    m = jnp.maximum(jnp.max(s, axis=-1, keepdims=True), sink)
    p = jnp.exp(s - m)
    return p / (jnp.sum(p, axis=-1, keepdims=True) + jnp.exp(sink - m))


def window_attn_prompt(q, k, v, sink):
    B, S = q.shape[0], q.shape[1]
    nb = S // CHUNK
    J = (WIN_CHUNKS + 1) * CHUNK
    pad = ((0, 0), (WIN_CHUNKS * CHUNK, 0), (0, 0), (0, 0))
    kp = jnp.pad(k, pad).reshape(B, nb + WIN_CHUNKS, CHUNK, N_KV_HEADS, HEAD_DIM)
    vp = jnp.pad(v, pad).reshape(B, nb + WIN_CHUNKS, CHUNK, N_KV_HEADS, HEAD_DIM)
    kb = jnp.concatenate([kp[:, j:j + nb] for j in range(WIN_CHUNKS + 1)], axis=2)
    vb = jnp.concatenate([vp[:, j:j + nb] for j in range(WIN_CHUNKS + 1)], axis=2)
    qb = q.reshape(B, nb, CHUNK, N_KV_HEADS, GROUP, HEAD_DIM)
    s = jnp.einsum('bnqkgd,bnjkd->bnkgqj', qb, kb, preferred_element_type=jnp.float32) * ATTN_SCALE
    qpos = jnp.arange(S, dtype=jnp.int32).reshape(nb, CHUNK)
    kpos = (jnp.arange(nb, dtype=jnp.int32)[:, None] - WIN_CHUNKS) * CHUNK + jnp.arange(J, dtype=jnp.int32)[None, :]
    dist = jnp.abs(qpos[:, :, None] - kpos[:, None, :]).astype(jnp.float32)
    slopes = alibi_slopes().reshape(N_KV_HEADS, GROUP)
    bias = -slopes[None, None, :, :, None, None] * dist[None, :, None, None, :, :]
    valid = (kpos >= 0)[None, :, None, None, None, :]
    s = jnp.where(valid, s + bias, jnp.finfo(jnp.float32).min)
    sk = sink.astype(jnp.float32).reshape(N_KV_HEADS, GROUP)[None, None, :, :, None, None]
    p = sink_softmax(s, sk)
    o = jnp.einsum('bnkgqj,bnjkd->bnqkgd', p.astype(v.dtype), vb)
    return o.reshape(B, S, Q_DIM)


def window_attn_sample(q, k_new, v_new, k_cache, v_cache, sink):
    Bd, n = q.shape[0], q.shape[1]
    rows = k_cache.shape[1]
    kk = jnp.concatenate([k_cache, k_new], axis=1)
    vv = jnp.concatenate([v_cache, v_new], axis=1)
    qg = q.reshape(Bd, n, N_KV_HEADS, GROUP, HEAD_DIM)
    s = jnp.einsum('bqkgd,bjkd->bkgqj', qg, kk, preferred_element_type=jnp.float32) * ATTN_SCALE
    qpos = PAST_LEN + jnp.arange(n, dtype=jnp.int32)
    kpos = PAST_LEN - rows + jnp.arange(rows + n, dtype=jnp.int32)
    dist = jnp.abs(qpos[:, None] - kpos[None, :]).astype(jnp.float32)
    slopes = alibi_slopes().reshape(N_KV_HEADS, GROUP)
    s = s - slopes[None, :, :, None, None] * dist[None, None, None]
    sk = sink.astype(jnp.float32).reshape(N_KV_HEADS, GROUP)[None, :, :, None, None]
    p = sink_softmax(s, sk)
    o = jnp.einsum('bkgqj,bjkd->bqkgd', p.astype(vv.dtype), vv)
    return o.reshape(Bd, n, Q_DIM)


def memory_kv(mem, g_mem, w_ckv):
    kv = rms_norm(mem, g_mem) @ w_ckv
    mk, mv = jnp.split(kv, 2, axis=-1)
    B = mem.shape[0]
    return (mk.reshape(B, N_MEM, MEM_HEADS, MEM_HEAD_DIM), mv.reshape(B, N_MEM, MEM_HEADS, MEM_HEAD_DIM))


def trunk_layer(x, attn_core, conv_left, mem_k, mem_v, g_mix, w_in, conv_w, w_attn_out, w_conv_out,
                w_mix_out, g_cross, w_cq, w_co, g_mlp, w_up, w_down):
    B, n = x.shape[0], x.shape[1]
    h = rms_norm(x, g_mix)
    z = h @ w_in
    q, k, v, ch, cb, cc, ga, gb = jnp.split(z, SPLITS, axis=-1)
    q = q.reshape(B, n, N_HEADS, HEAD_DIM)
    k = k.reshape(B, n, N_KV_HEADS, HEAD_DIM)
    v = v.reshape(B, n, N_KV_HEADS, HEAD_DIM)
    a = attn_core(q, k, v) @ w_attn_out
    u = cc * ch
    up = jnp.concatenate([conv_left, u], axis=1)
    conv = up[:, 0:n] * conv_w[0]
    for j in range(1, CONV_WIDTH):
        conv = conv + up[:, j:j + n] * conv_w[j]
    bo = (cb * conv) @ w_conv_out
    x = x + (jax.nn.sigmoid(ga) * a + jax.nn.sigmoid(gb) * bo) @ w_mix_out
    hc = rms_norm(x, g_cross)
    cq = (hc @ w_cq).reshape(B, n, MEM_HEADS, MEM_HEAD_DIM)
    s = jnp.einsum('bqhd,bmhd->bhqm', cq, mem_k, preferred_element_type=jnp.float32) * MEM_SCALE
    p = jax.nn.softmax(s, axis=-1)
    co = jnp.einsum('bhqm,bmhd->bqhd', p.astype(mem_v.dtype), mem_v).reshape(B, n, MEM_DIM)
    x = x + co @ w_co
    hm = rms_norm(x, g_mlp)
    x = x + jnp.square(jax.nn.relu(hm @ w_up)) @ w_down
    return x, k, v, up[:, -(CONV_WIDTH - 1):]


def setup_inputs(seed: int = 0) -> dict:
    key = jax.random.key(seed)
    ks = jax.random.split(key, 24)
    f32 = jnp.float32
    win_rows = min(WINDOW, PAST_LEN)

    def nrm(k, shape, scale):
        return jax.random.normal(k, shape, f32) * scale

    def gain(k, shape):
        return 1.0 + 0.05 * jax.random.normal(k, shape, f32)

    return {
        'x_prompt': nrm(ks[0], (BATCH, SEQ, D_MODEL), 1.0),
        'x_sample': nrm(ks[1], (DEC_BATCH, DEC_SEQ, D_MODEL), 1.0),
        'mem_prompt': nrm(ks[2], (BATCH, N_MEM, D_MODEL), 1.0),
        'cache_attn_k': nrm(ks[3], (DEPTH, DEC_BATCH, win_rows, N_KV_HEADS, HEAD_DIM), 1.0),
        'cache_attn_v': nrm(ks[4], (DEPTH, DEC_BATCH, win_rows, N_KV_HEADS, HEAD_DIM), 1.0),
        'state_conv': nrm(ks[5], (DEPTH, DEC_BATCH, CONV_WIDTH - 1, CONV_DIM), 1.0),
        'cache_mem_k': nrm(ks[6], (DEPTH, DEC_BATCH, N_MEM, MEM_HEADS, MEM_HEAD_DIM), 1.0),
        'cache_mem_v': nrm(ks[7], (DEPTH, DEC_BATCH, N_MEM, MEM_HEADS, MEM_HEAD_DIM), 1.0),
        'norm_mix_g': gain(ks[8], (DEPTH, D_MODEL)),
        'w_in': nrm(ks[9], (DEPTH, D_MODEL, IN_COLS), D_MODEL ** -0.5),
        'conv_w': nrm(ks[10], (DEPTH, CONV_WIDTH, CONV_DIM), CONV_WIDTH ** -0.5),
        'attn_sink': nrm(ks[11], (DEPTH, N_HEADS), 0.5),
        'w_attn_out': nrm(ks[12], (DEPTH, Q_DIM, D_MODEL), Q_DIM ** -0.5),
        'w_conv_out': nrm(ks[13], (DEPTH, CONV_DIM, D_MODEL), CONV_DIM ** -0.5),
        'w_mix_out': nrm(ks[14], (DEPTH, D_MODEL, D_MODEL), D_MODEL ** -0.5),
        'norm_cross_g': gain(ks[15], (DEPTH, D_MODEL)),
        'norm_mem_g': gain(ks[16], (DEPTH, D_MODEL)),
        'w_cq': nrm(ks[17], (DEPTH, D_MODEL, MEM_DIM), D_MODEL ** -0.5),
        'w_ckv': nrm(ks[18], (DEPTH, D_MODEL, 2 * MEM_DIM), D_MODEL ** -0.5),
        'w_co': nrm(ks[19], (DEPTH, MEM_DIM, D_MODEL), MEM_DIM ** -0.5),
        'norm_mlp_g': gain(ks[20], (DEPTH, D_MODEL)),
        'w_up': nrm(ks[21], (DEPTH, D_MODEL, D_FF), D_MODEL ** -0.5),
        'w_down': nrm(ks[22], (DEPTH, D_FF, D_MODEL), 0.5 * D_FF ** -0.5),
        'norm_final_g': gain(ks[23], (D_MODEL,)),
    }


def reference(x_prompt, x_sample, mem_prompt, cache_attn_k, cache_attn_v, state_conv, cache_mem_k, cache_mem_v,
              norm_mix_g, w_in, conv_w, attn_sink, w_attn_out, w_conv_out, w_mix_out, norm_cross_g, norm_mem_g,
              w_cq, w_ckv, w_co, norm_mlp_g, w_up, w_down, norm_final_g):
    xp = x_prompt
    xs = x_sample
    prompt_rows = min(WINDOW, xp.shape[1])
    kp_l, vp_l, cp_l, mkp_l, mvp_l = [], [], [], [], []
    ks_l, vs_l, cs_l = [], [], []
    for l in range(DEPTH):
        sink = attn_sink[l]
        shared = (norm_mix_g[l], w_in[l], conv_w[l], w_attn_out[l], w_conv_out[l], w_mix_out[l],
                  norm_cross_g[l], w_cq[l], w_co[l], norm_mlp_g[l], w_up[l], w_down[l])
        mk, mv = memory_kv(mem_prompt, norm_mem_g[l], w_ckv[l])
        zero_left = jnp.zeros((xp.shape[0], CONV_WIDTH - 1, CONV_DIM), xp.dtype)
        xp, kp, vp, cpst = trunk_layer(xp, lambda q, k, v: window_attn_prompt(q, k, v, sink),
                                       zero_left, mk, mv, *shared)
        kp_l.append(kp[:, -prompt_rows:])
        vp_l.append(vp[:, -prompt_rows:])
        cp_l.append(cpst)
        mkp_l.append(mk)
        mvp_l.append(mv)
        kc, vc = cache_attn_k[l], cache_attn_v[l]
        xs, kn, vn, csst = trunk_layer(xs, lambda q, k, v: window_attn_sample(q, k, v, kc, vc, sink),
                                       state_conv[l], cache_mem_k[l], cache_mem_v[l], *shared)
        ks_l.append(kn)
        vs_l.append(vn)
        cs_l.append(csst)
    y_prompt = rms_norm(xp, norm_final_g)
    y_sample = rms_norm(xs, norm_final_g)
    return (y_prompt, y_sample,
            jnp.stack(kp_l), jnp.stack(vp_l), jnp.stack(cp_l), jnp.stack(mkp_l), jnp.stack(mvp_l),
            jnp.stack(ks_l), jnp.stack(vs_l), jnp.stack(cs_l))
```

```python
import contextlib
import numpy as np
import concourse.bass as bass
import concourse.mybir as mybir
from concourse.bass_utils import run_bass_kernel_spmd

F32 = mybir.dt.float32
BF16 = mybir.dt.bfloat16
ALU = mybir.AluOpType
AF = mybir.ActivationFunctionType

ENGS = ("pe", "act", "dve", "pool", "sp")
COMPUTE = ("pe", "act", "dve", "pool")


class Op:
    __slots__ = ("eng", "fn", "waits", "signal", "semval", "dma", "sem", "clock", "idx", "inc")


class DSem:
    def __init__(self, sem):
        self.sem = sem
        self.count = 0
        self.last = None


class Prog:
    def __init__(self, nc):
        self.nc = nc
        self.ops = {e: [] for e in ENGS}
        self.res = {}
        self.run = {e: {} for e in ENGS}
        self.cnt = {e: 0 for e in ENGS}
        self.ndma = 0
        self.dma_ops = []
        self.lastc = {}

    def _need(self, op, d, raw, hard=True):
        if d is None:
            return
        run = self.run[op.eng]
        if d.dma:
            if run.get(("d", d.idx), 0) >= 1:
                return
        else:
            if d.eng == op.eng and not op.dma:
                if op.eng == "pe" or not hard:
                    return
            if run.get(d.eng, 0) >= d.idx:
                return
        d.signal = True
        op.waits.append(d)
        for k, v in d.clock.items():
            if run.get(k, 0) < v:
                run[k] = v

    def _mk(self, eng, fn, dma):
        o = Op()
        o.eng = eng
        o.fn = fn
        o.waits = []
        o.signal = False
        o.dma = dma
        o.sem = None
        o.semval = None
        o.inc = 1
        return o

    def op(self, eng, fn, reads=(), writes=(), dma=False, dsem=None):
        o = self._mk(eng, fn, dma)
        for r in reads:
            e = self.res.get(r)
            if e is not None:
                self._need(o, e[0], True)
        for w in writes:
            e = self.res.get(w)
            if e is not None:
                hard = not (isinstance(w, tuple) and w[0] in ("ps", "pt"))
                self._need(o, e[0], False, hard)
                for rd in reversed(e[1]):
                    self._need(o, rd, False, hard)
        if dma:
            self._need(o, dsem.last, False)
            self.ndma += 1
            o.idx = self.ndma
            dsem.count += 1
            dsem.last = o
            o.sem = dsem.sem
            o.semval = 16 * dsem.count
            o.inc = 16
            o.signal = True
            o.clock = dict(self.run[eng])
            o.clock[("d", o.idx)] = 1
            self.dma_ops.append(o)
        else:
            self.cnt[eng] += 1
            o.idx = self.cnt[eng]
            o.clock = dict(self.run[eng])
            o.clock[eng] = o.idx
            self.lastc[eng] = o
        for r in reads:
            e = self.res.get(r)
            if e is None:
                self.res[r] = [None, [o]]
            else:
                e[1].append(o)
        for w in writes:
            self.res[w] = [o, []]
        self.ops[eng].append(o)
        return o

    def _pseudo(self, eng, deps):
        o = self._mk(eng, None, False)
        self.cnt[eng] += 1
        o.idx = self.cnt[eng]
        for d in deps:
            self._need(o, d, True)
        o.clock = dict(self.run[eng])
        self.ops[eng].append(o)

    def barrier(self):
        last = dict(self.lastc)
        for e in COMPUTE:
            self._pseudo(e, [d for e2, d in last.items() if e2 != e])

    def wait_all_dma(self, eng):
        self._pseudo(eng, list(self.dma_ops))

    def emit(self, esems):
        nc = self.nc
        for e in ENGS:
            c = 0
            for o in self.ops[e]:
                if o.dma:
                    continue
                if o.signal:
                    assert o.fn is not None
                    c += 1
                    o.sem = esems[e]
                    o.semval = c
        stats = {}
        with nc.Block() as block:
            def run(engname, eng):
                nw = 0
                for o in self.ops[engname]:
                    for d in o.waits:
                        eng.wait_ge(d.sem, d.semval)
                        nw += 1
                    if o.fn is None:
                        continue
                    inst = o.fn(eng)
                    if o.signal:
                        inst.then_inc(o.sem, o.inc)
                stats[engname] = (len(self.ops[engname]), nw)

            @block.tensor
            def _(eng):
                run("pe", eng)

            @block.scalar
            def _(eng):
                run("act", eng)

            @block.vector
            def _(eng):
                run("dve", eng)

            @block.gpsimd
            def _(eng):
                run("pool", eng)

            @block.sync
            def _(eng):
                run("sp", eng)
        return stats


L = 4
EPS = 1e-6
SLOPES = [float(2.0 ** (-8.0 * (h + 1) / 16.0)) for h in range(16)]
NEGBIG = -30000.0
NS = 4
TMAX = 1152
CQ, CK, CV, CCH, CCB, CCC, CGA, CGB = 0, 1024, 1280, 1536, 2560, 3584, 4608, 5632
QPERM_HEADS = []
for _c in range(8):
    _a = [0, 1, 2, 3, 8, 9, 10, 11][_c]
    QPERM_HEADS += [_a, _a + 4]


def build(nlayers=L, npass=2):
    nc = bass.Bass("TRN2", target_bir_lowering=False)

    def din(name, shape):
        return nc.dram_tensor(name, shape, F32, kind="ExternalInput").ap()

    def dout(name, shape):
        return nc.dram_tensor(name, shape, F32, kind="ExternalOutput").ap()

    xp = din("xp", [2048, 1024])
    xs = din("xs", [128, 1024])
    mem = din("mem", [256, 1024])
    cak = din("cak", [4, 4, 128, 256])
    cav = din("cav", [4, 4, 128, 256])
    sconv = din("sconv", [4, 8, 1024])
    cmk = din("cmk", [4, 4, 256, 256])
    cmv = din("cmv", [4, 4, 256, 256])
    gall = din("gall", [29, 1024])
    sink = din("sink", [4, 16])
    w_in = din("w_in", [4, 1024, 6656])
    wao = din("wao", [4, 1024, 1024])
    wco = din("wco", [4, 1024, 1024])
    wmix = din("wmix", [4, 1024, 1024])
    wcq = din("wcq", [4, 1024, 256])
    wckv = din("wckv", [4, 1024, 512])
    wcox = din("wcox", [4, 256, 1024])
    wup = din("wup", [4, 1024, 4096])
    wdown = din("wdown", [4, 4096, 1024])
    yp = dout("yp", [2048, 1024])
    ys = dout("ys", [128, 1024])
    kp = dout("kp", [4, 128, 256])
    vp = dout("vp", [4, 128, 256])
    convp = dout("convp", [4, 2, 1024])
    mkp = dout("mkp", [4, 256, 256])
    mvp = dout("mvp", [4, 256, 256])
    ksn = dout("ksn", [4, 128, 256])
    vsn = dout("vsn", [4, 128, 256])
    convs = dout("convs", [4, 8, 1024])

    P = Prog(nc)
    with contextlib.ExitStack() as st:
        def sb(name, shape, dt):
            return st.enter_context(nc.sbuf_tensor(name, shape, dt))

        esems = {e: st.enter_context(nc.semaphore("s_" + e)) for e in ENGS}
        wsems = [DSem(st.enter_context(nc.semaphore("w%d" % i))) for i in range(NS)]
        msems = [DSem(st.enter_context(nc.semaphore("m%d" % i))) for i in range(24)]
        mcount = [0]

        psems = [DSem(st.enter_context(nc.semaphore("q%d" % i))) for i in range(16)]
        pcount = [0]

        def ds(eng="sp"):
            if eng == "pool":
                pcount[0] += 1
                return psems[pcount[0] % len(psems)]
            mcount[0] += 1
            return msems[mcount[0] % len(msems)]

        ident = sb("ident", [128, 128], F32)
        identb = sb("identb", [128, 128], BF16)
        onesb = sb("onesb", [128, 128], BF16)
        gT = sb("gT", [128, 8, 29], F32)
        esink = sb("esink", [128, 64], F32)
        ndP = sb("ndP", [128, 2, 128], F32)
        ndS = sb("ndS", [128, 2, 128], F32)
        xT = sb("xT", [128, 8, TMAX], F32)
        hb = sb("hb", [128, 8, TMAX], BF16)
        NA = 32384
        arena = sb("arena", [128, NA], BF16)
        wslot = [sb("wslot%d" % i, [128, 4096], BF16) for i in range(NS)]
        xtok = sb("xtok", [128, 1024], F32)
        gtok = xtok[0:29, :]
        sctok = xtok[0:8, :]
        tA = sb("tA", [128, 512], F32)
        tB = sb("tB", [128, 512], F32)
        tC = sb("tC", [128, 512], F32)
        tD = sb("tD", [128, 512], F32)
        sq = sb("sq", [128, 2, 512], BF16)
        ubuf = sb("ubuf", [128, 1026], F32)
        usb = sb("usb", [128, 4, 34], F32)
        PT = sb("PT", [128, 3, 1024], BF16)
        atok = sb("atok", [128, 2, 1024], BF16)
        den = sb("den", [128, 2, 4], F32)
        epsT = sb("epsT", [128, 1], F32)
        kst = sb("kst", [128, 4, 2, 128], BF16)
        vst = sb("vst", [128, 4, 4, 65], BF16)
        ust = sb("ust", [128, 4, 8, 2], F32)
        mkTp = sb("mkTp", [128, 4, 2, 256], BF16)
        mvp_sb = sb("mvp_sb", [128, 4, 2, 4, 65], BF16)
        tinA = sb("tinA", [128, 4, 256], BF16)
        kcT = sb("kcT", [128, 4, 2, 128], BF16)
        vc = sb("vc", [128, 4, 4, 65], BF16)
        vnew = sb("vnew", [128, 4, 4, 65], BF16)
        scT = sb("scT", [128, 8, 8], F32)
        cvoS = sb("cvoS", [128, 8, 8], F32)
        cvoP = sb("cvoP", [128, 8, 2], F32)
        pf = [st.enter_context(nc.psum_tensor("pf%d" % i, [128, 512], F32)) for i in range(6)]
        pbt = [st.enter_context(nc.psum_tensor("pbt%d" % i, [128, 1024], BF16)) for i in range(2)]

        def av(off, n):
            return arena[:, off:off + n]

        m_ = av(0, 9216).rearrange("p (c t) -> p c t", c=8)
        qa = av(9216, 9216).rearrange("p (c t) -> p c t", c=8)
        kT = av(18432, 2304).rearrange("p (c t) -> p c t", c=2)
        vsb = av(20736, 2432)[:, 0:9 * 260].rearrange("p (t g d) -> p t g d", t=9, g=4)
        cbc = av(23168, 9216).rearrange("p (c t) -> p c t", c=8)
        atoks = arena[0:32, 0:4096]
        cqT = av(9216, 2304).rearrange("p (c t) -> p c t", c=2)
        PTx = av(11520, 4096).rearrange("p (s n) -> p s n", s=8)
        coT = av(15616, 2304).rearrange("p (c t) -> p c t", c=2)
        mkTs = av(17920, 2048).rearrange("p (b c k) -> p b c k", b=4, c=2)
        mvs = av(19968, 2080).rearrange("p (b k h d) -> p b k h d", b=4, k=2, h=4)
        tinM = av(22048, 2048).rearrange("p (b k f) -> p b k f", b=4, k=2)
        PTx2 = av(24096, 4096).rearrange("p (s n) -> p s n", s=8)
        hid = av(0, 16 * TMAX).rearrange("p (c t) -> p c t", c=16)

        def OP(eng, meth, reads, writes, *a, **k):
            P.op(eng, lambda e: getattr(e, meth)(*a, **k), reads=reads, writes=writes)

        def DMA(eng, out, in_, reads, writes, **k):
            P.op(eng, lambda e: e.dma_start(out=out, in_=in_, **k), reads=reads, writes=writes, dma=True, dsem=ds(eng))

        def MM(bank, out, lhsT, rhs, start, stop, reads):
            P.op("pe", lambda e: e.matmul(out, lhsT=lhsT, rhs=rhs, start=start, stop=stop),
                 reads=reads, writes=[("ps", bank)])

        def TRF(bank, out, in_, idn, reads):
            P.op("pe", lambda e: e.transpose(out, in_, idn), reads=reads, writes=[("ps", bank)])

        def TRB(bank, out, in_, idn, reads):
            P.op("pe", lambda e: e.transpose(out, in_, idn), reads=reads, writes=[("pt", bank)])

        bankc = [0]
        stepc = [0]

        bmode = ["all"]
        bankS = [0]

        def nb():
            if bmode[0] == "split":
                bankc[0] = (bankc[0] + 1) % 3
            elif bmode[0] == "split2":
                bankc[0] = (bankc[0] + 1) % 2
            else:
                bankc[0] = (bankc[0] + 1) % 6
            return bankc[0]

        def nbS():
            if bmode[0] == "split":
                bankS[0] = (bankS[0] + 1) % 2
                return 3 + bankS[0]
            if bmode[0] == "split2":
                bankS[0] = (bankS[0] + 1) % 3
                return 2 + bankS[0]
            return nb()

        def nbO():
            if bmode[0] in ("split", "split2"):
                return 5
            return nb()

        btc = [0]

        def nbt():
            btc[0] = (btc[0] + 1) % 2
            return btc[0]

        def keys(name, cs, tiles):
            return [(name, c, t) for c in cs for t in tiles]

        def gt(n0, n1):
            return range(n0 // 128, (n1 + 127) // 128)

        units = []

        def U(tag, src, KC, NCc):
            units.append((tag, src, KC, NCc))

        def wsrc(w2d, r0, kc, c0, ncols):
            return w2d[r0:r0 + kc * 128, c0:c0 + ncols].rearrange("(k p) n -> p k n", p=128)

        for l in range(nlayers):
            U("ckv", wsrc(wckv[l], 0, 8, 0, 512), 8, 512)
        for pas in range(npass):
            for l in range(nlayers):
                for qh in range(2):
                    U("q", wsrc(w_in[l], 0, 8, CQ + qh * 512, 512), 8, 512)
                U("kv", wsrc(w_in[l], 0, 8, CK, 512), 8, 512)
                for fg in range(2):
                    U("ch", wsrc(w_in[l], 0, 8, CCH + fg * 512, 512), 8, 512)
                    U("cc", wsrc(w_in[l], 0, 8, CCC + fg * 512, 512), 8, 512)
                    U("cb", wsrc(w_in[l], 0, 8, CCB + fg * 512, 512), 8, 512)
                for hf in range(2):
                    U("gb", wsrc(w_in[l], 0, 8, CGB + hf * 512, 512), 8, 512)
                    U("wco", wsrc(wco[l], 0, 8, hf * 512, 512), 8, 512)
                for hf in range(2):
                    U("ga", wsrc(w_in[l], 0, 8, CGA + hf * 512, 512), 8, 512)
                    U("wao", wsrc(wao[l], 0, 8, hf * 512, 512), 8, 512)
                for hf in range(2):
                    U("wmix", wsrc(wmix[l], 0, 8, hf * 512, 512), 8, 512)
                U("wcq", wsrc(wcq[l], 0, 8, 0, 256), 8, 256)
                U("wcox", wsrc(wcox[l], 0, 2, 0, 1024), 2, 1024)
                for hf in range(2):
                    for uu in range(4):
                        U("up", wsrc(wup[l], 0, 8, (hf * 4 + uu) * 512, 512), 8, 512)
                    for j in range(8):
                        U("down", wsrc(wdown[l], hf * 2048, 16, j * 128, 128), 16, 128)
        wstate = {"issued": 0, "used": 0}

        def issue_to(n):
            while wstate["issued"] < min(n, len(units)):
                i = wstate["issued"]
                tag, src, KC, NCc = units[i]
                s = i % NS
                dst = wslot[s][:, 0:KC * NCc].rearrange("p (k n) -> p k n", k=KC)
                P.op("pool", lambda e, dst=dst, src=src: e.dma_start(out=dst, in_=src),
                     writes=[("w", s)], dma=True, dsem=wsems[s])
                wstate["issued"] += 1

        def release(n):
            issue_to(wstate["used"] + n)

        def next_unit(tag, keep=1):
            i = wstate["used"]
            assert units[i][0] == tag, (units[i][0], tag)
            wstate["used"] += 1
            issue_to(i - (keep - 1) + NS)
            s = i % NS
            KC, NCc = units[i][2], units[i][3]
            return wslot[s][:, 0:KC * NCc].rearrange("p (k n) -> p k n", k=KC), ("w", s)

        OP("pool", "memset", [], ["ident"], ident[:], 1.0)
        OP("pool", "affine_select", ["ident"], ["ident"], out=ident[:], in_=ident[:], pattern=[[-1, 128]],
           compare_op=ALU.is_equal, fill=0.0, base=0, channel_multiplier=1)
        OP("dve", "tensor_copy", ["ident"], ["identb"], identb[:], ident[:])
        OP("pool", "memset", [], ["onesb"], onesb[:], 1.0)
        OP("pool", "memset", [], ["epsT"], epsT[:], EPS)
        OP("pool", "memset", [], ["u"], ubuf[:], 0.0)
        OP("pool", "memset", [], ["us"], usb[:], 0.0)
        OP("pool", "memset", [], ["vst"], vst[:], 1.0)
        OP("pool", "memset", [], ["mvp_sb"], mvp_sb[:], 1.0)
        OP("pool", "memset", [], ["vc"], vc[:], 1.0)
        OP("pool", "memset", [], ["vnew"], vnew[:], 1.0)
        DMA("sp", gtok, gall, [], ["xtok0", "xtok1"])
        DMA("sp", esink[:], sink.rearrange("l h -> (l h)").partition_broadcast(128), [], ["esink"])
        OP("act", "activation", ["esink"], ["esink"], out=esink[:], in_=esink[:], func=AF.Exp)
        b0 = nb()
        for c in range(8):
            TRF(b0, pf[b0][:, c * 29:(c + 1) * 29], gtok[:, c * 128:(c + 1) * 128], ident[0:29, 0:29], ["xtok0", "xtok1", "ident"])
        OP("act", "activation", [], [("ps", b0), "gT"], out=gT[:].rearrange("p a b -> p (a b)"), in_=pf[b0][:, 0:232], func=AF.Copy)
        OP("pool", "iota", [], ["ndP"], ndP[:, 0, :], pattern=[[1, 128]], base=128, channel_multiplier=-1, allow_small_or_imprecise_dtypes=True)
        OP("pool", "iota", ["ndP"], ["ndP"], ndP[:, 1, :], pattern=[[1, 128]], base=0, channel_multiplier=-1, allow_small_or_imprecise_dtypes=True)
        OP("act", "activation", ["ndP"], ["ndP"], out=ndP[:], in_=ndP[:], func=AF.Abs)
        OP("dve", "tensor_scalar", ["ndP"], ["ndP"], out=ndP[:], in0=ndP[:], scalar1=-1.0, scalar2=None, op0=ALU.mult)
        OP("pool", "memset", ["ndP"], ["ndP"], ndP[0:64, 0, 64:128], NEGBIG)
        OP("pool", "memset", ["ndP"], ["ndP"], ndP[64:128, 1, 0:64], NEGBIG)
        OP("pool", "iota", [], ["ndS"], ndS[:, 0, :].rearrange("p (b i) -> p b i", b=4), pattern=[[0, 4], [1, 32]], base=128, channel_multiplier=-1, allow_small_or_imprecise_dtypes=True)
        OP("pool", "iota", ["ndS"], ["ndS"], ndS[:, 1, :].rearrange("p (b i) -> p b i", b=4), pattern=[[0, 4], [1, 32]], base=0, channel_multiplier=-1, allow_small_or_imprecise_dtypes=True)
        OP("act", "activation", ["ndS"], ["ndS"], out=ndS[:], in_=ndS[:], func=AF.Abs)
        OP("dve", "tensor_scalar", ["ndS"], ["ndS"], out=ndS[:], in0=ndS[:], scalar1=-1.0, scalar2=None, op0=ALU.mult)

        def rms(gi, groups, src=xT, dst=hb, xname="x", hname="h", toff=0, hoff=0):
            for (n0, n1) in groups:
                N = n1 - n0
                tl = list(gt(n0, n1))
                b = nb()
                for c in range(8):
                    s2 = c % 2
                    OP("act", "activation", keys(xname, [c], tl), [("sq", s2)], out=sq[:, s2, 0:N], in_=src[:, c, n0:n1], func=AF.Square)
                    MM(b, pf[b][:, 0:N], onesb[:], sq[:, s2, 0:N], c == 0, c == 7, [("sq", s2), "onesb"])
                OP("act", "activation", ["epsT"], [("ps", b), "tA"], out=tA[:, 0:N], in_=pf[b][:, 0:N], func=AF.Ln, scale=1.0 / 1024.0, bias=epsT[:])
                OP("act", "activation", ["tA"], ["tA"], out=tA[:, 0:N], in_=tA[:, 0:N], func=AF.Exp, scale=-0.5)
                for c in range(8):
                    OP("dve", "scalar_tensor_tensor", keys(xname, [c], tl) + ["tA", "gT"], keys(hname, [c], [t + hoff for t in tl]),
                       out=dst[:, c, n0 + hoff * 128:n1 + hoff * 128], in0=src[:, c, n0:n1], scalar=gT[:, c, gi:gi + 1], in1=tA[:, 0:N], op0=ALU.mult, op1=ALU.mult)

        import os as _os
        _STOP = int(_os.environ.get('KSTOP', '0'))

        class _Stop(Exception):
            pass

        def stage(n):
            if n == _STOP:
                raise _Stop()

        def _body():
            def xload(ti, kind, gi_):
                src = xp[gi_ * 128:(gi_ + 1) * 128, :] if kind == "p" else xs
                for half in range(2):
                    hk = "xtok%d" % half
                    DMA("sp", xtok[:, half * 512:(half + 1) * 512], src[:, half * 512:(half + 1) * 512], [], [hk])
                    b = nb()
                    for i in range(4):
                        c = half * 4 + i
                        TRF(b, pf[b][:, i * 128:(i + 1) * 128], xtok[:, c * 128:(c + 1) * 128], ident[:], [hk, "ident"])
                    OP("act", "activation", [], [("ps", b)] + keys("x", range(half * 4, half * 4 + 4), [ti]),
                       out=xT[:, half * 4:half * 4 + 4, ti * 128:(ti + 1) * 128], in_=pf[b][:].rearrange("p (c t) -> p c t", c=4), func=AF.Copy)

            NPRE = 5
            for ti in range(NPRE):
                xload(ti, "p", ti)
            for kt in range(2):
                DMA("sp", xtok[:], mem[kt * 128:(kt + 1) * 128, :], [], ["xtok0", "xtok1"])
                for half in range(2):
                    b = nb()
                    for i in range(4):
                        c = half * 4 + i
                        TRF(b, pf[b][:, i * 128:(i + 1) * 128], xtok[:, c * 128:(c + 1) * 128], ident[:], ["xtok0", "xtok1", "ident"])
                    OP("act", "activation", [], [("ps", b)] + keys("x", range(half * 4, half * 4 + 4), [5 + kt]),
                       out=xT[:, half * 4:half * 4 + 4, (5 + kt) * 128:(6 + kt) * 128], in_=pf[b][:].rearrange("p (c t) -> p c t", c=4), func=AF.Copy)
            if True:
                N = 256
                b = nb()
                for c in range(8):
                    s2 = c % 2
                    OP("act", "activation", keys("x", [c], [5, 6]), [("sq", s2)], out=sq[:, s2, 0:N], in_=xT[:, c, 640:896], func=AF.Square)
                    MM(b, pf[b][:, 0:N], onesb[:], sq[:, s2, 0:N], c == 0, c == 7, [("sq", s2), "onesb"])
                OP("act", "activation", ["epsT"], [("ps", b), "tA"], out=tA[:, 0:N], in_=pf[b][:, 0:N], func=AF.Ln, scale=1.0 / 1024.0, bias=epsT[:])
                OP("act", "activation", ["tA"], ["tA"], out=tA[:, 0:N], in_=tA[:, 0:N], func=AF.Exp, scale=-0.5)
                for c in range(8):
                    OP("dve", "tensor_tensor", keys("x", [c], [5, 6]) + ["tA"], keys("x", [c], [7, 8]),
                       out=xT[:, c, 896:1152], in0=xT[:, c, 640:896], in1=tA[:, 0:N], op=ALU.mult)
            for l in range(nlayers):
                ho = 640 + (l % 2) * 256
                ht = [5 + (l % 2) * 2, 6 + (l % 2) * 2]
                for c in range(8):
                    OP("dve", "tensor_scalar", keys("x", [c], [7, 8]) + ["gT"], keys("h", [c], ht),
                       out=hb[:, c, ho:ho + 256], in0=xT[:, c, 896:1152], scalar1=gT[:, c, 12 + l:13 + l], scalar2=None, op0=ALU.mult)
                W, wk = next_unit("ckv")
                for kt in range(2):
                    b = nb()
                    for kc in range(8):
                        MM(b, pf[b][:, 0:512], hb[:, kc, ho + kt * 128:ho + (kt + 1) * 128], W[:, kc, :], kc == 0, kc == 7, keys("h", [kc], ht) + [wk])
                    OP("act", "activation", [], [("ps", b), "tD"], out=tD[:], in_=pf[b][:], func=AF.Copy)
                    OP("dve", "tensor_copy", ["tD"], ["mvp_sb"], mvp_sb[:, l, kt, :, 0:64], tD[:, 256:512].rearrange("p (h d) -> p h d", h=4))
                    DMA("sp", mkp[l, kt * 128:(kt + 1) * 128, :], tD[:, 0:256], ["tD"], [])
                    DMA("sp", mvp[l, kt * 128:(kt + 1) * 128, :], tD[:, 256:512], ["tD"], [])
                for c in range(2):
                    b = nb()
                    for kc in range(8):
                        MM(b, pf[b][:, 0:256], W[:, kc, c * 128:(c + 1) * 128], hb[:, kc, ho:ho + 256], kc == 0, kc == 7, keys("h", [kc], ht) + [wk])
                    OP("act", "activation", [], [("ps", b), "mkTp"], out=mkTp[:, l, c, :], in_=pf[b][:, 0:256], func=AF.Copy)
            stage(1)

            for pas in range(npass):
                if pas == 0:
                    tiles = [("p", i) for i in range(8)] + [("s", 0)]
                else:
                    tiles = [("p", i) for i in range(8, 16)]
                NT = len(tiles)
                T = NT * 128
                groups = [(n0, min(n0 + 512, T)) for n0 in range(0, T, 512)]
                has_s = (pas == 0)
                s_ti = 8

                for ti, (kind, gi_) in enumerate(tiles):
                    if pas == 0 and ti < NPRE:
                        continue
                    xload(ti, kind, gi_)

                stage(2)
                for l in range(nlayers):
                    if has_s:
                        P.op("pool", lambda e, l=l: e.dma_start(out=tinA[:], in_=cak[l].rearrange("b p f -> p b f")), writes=["tinA"], dma=True, dsem=ds("pool"))
                        for b4 in range(4):
                            P.op("pool", lambda e, l=l, b4=b4: e.dma_start(out=vc[:, b4, :, 0:64], in_=cav[l, b4].rearrange("p (g d) -> p g d", g=4)),
                                 writes=[("vc", b4)], dma=True, dsem=ds("pool"))
                        DMA("sp", sctok, sconv[l], [], ["xtok0", "xtok1"])
                        b = nb()
                        for j in range(8):
                            TRF(b, pf[b][:, j * 8:(j + 1) * 8], sctok[:, j * 128:(j + 1) * 128], ident[0:8, 0:8], ["xtok0", "xtok1", "ident"])
                        OP("act", "activation", [], [("ps", b), "scT"], out=scT[:].rearrange("p a b -> p (a b)"), in_=pf[b][:, 0:64], func=AF.Copy)
                        for b4 in range(4):
                            bt = nbt()
                            for c in range(2):
                                TRB(bt, pbt[bt][:, c * 128:(c + 1) * 128], tinA[:, b4, c * 128:(c + 1) * 128], identb[:], ["tinA", "identb"])
                            OP("act", "activation", [], [("pt", bt), ("kcT", b4)], out=kcT[:, b4, :, :].rearrange("p c k -> p (c k)"), in_=pbt[bt][:, 0:256], func=AF.Copy)

                    stage(3)
                    if l == 0:
                        rms(l, groups)
                    stage(4)

                    for qh in range(2):
                        Wq, kq = next_unit("q")
                        for jj in range(4):
                            cq_ = qh * 4 + jj
                            for (n0, n1) in groups:
                                N = n1 - n0
                                tl = list(gt(n0, n1))
                                b = nb()
                                for kc in range(8):
                                    MM(b, pf[b][:, 0:N], Wq[:, kc, jj * 128:(jj + 1) * 128], hb[:, kc, n0:n1], kc == 0, kc == 7, keys("h", [kc], tl) + [kq])
                                OP("act", "activation", [], [("ps", b)] + keys("qa", [cq_], tl), out=qa[:, cq_, n0:n1], in_=pf[b][:, 0:N], func=AF.Copy, scale=0.125)
                    Wkv, kkv = next_unit("kv")
                    for c in range(2):
                        for (n0, n1) in groups:
                            N = n1 - n0
                            tl = list(gt(n0, n1))
                            b = nb()
                            for kc in range(8):
                                MM(b, pf[b][:, 0:N], Wkv[:, kc, c * 128:(c + 1) * 128], hb[:, kc, n0:n1], kc == 0, kc == 7, keys("h", [kc], tl) + [kkv])
                            OP("dve", "tensor_copy", [], [("ps", b)] + keys("kT", [c], tl), kT[:, c, n0:n1], pf[b][:, 0:N])
                    for ti, (kind, gi_) in enumerate(tiles):
                        need_out = (kind == "s") or (gi_ == 15)
                        b = nb()
                        if need_out:
                            for kc in range(8):
                                MM(b, pf[b][:, 0:512], hb[:, kc, ti * 128:(ti + 1) * 128], Wkv[:, kc, 0:512], kc == 0, kc == 7, keys("h", [kc], [ti]) + [kkv])
                        else:
                            for kc in range(8):
                                MM(b, pf[b][:, 256:512], hb[:, kc, ti * 128:(ti + 1) * 128], Wkv[:, kc, 256:512], kc == 0, kc == 7, keys("h", [kc], [ti]) + [kkv])
                        OP("dve", "tensor_copy", [], [("ps", b), ("v", ti)], vsb[:, ti, :, 0:64], pf[b][:, 256:512].rearrange("p (g d) -> p g d", g=4))
                        if need_out:
                            OP("act", "activation", [], [("ps", b), "tD"], out=tD[:], in_=pf[b][:], func=AF.Copy)
                            if kind == "s":
                                DMA("sp", ksn[l], tD[:, 0:256], ["tD"], [])
                                DMA("sp", vsn[l], tD[:, 256:512], ["tD"], [])
                            else:
                                DMA("sp", kp[l], tD[:, 0:256], ["tD"], [])
                                DMA("sp", vp[l], tD[:, 256:512], ["tD"], [])
                        if kind == "s":
                            for b4 in range(4):
                                bb = nb()
                                for kc in range(8):
                                    MM(bb, pf[bb][0:32, 0:256], hb[:, kc, ti * 128 + b4 * 32:ti * 128 + (b4 + 1) * 32], Wkv[:, kc, 256:512], kc == 0, kc == 7, keys("h", [kc], [ti]) + [kkv])
                                OP("dve", "tensor_copy", [], [("ps", bb), ("vnew", b4)], vnew[0:32, b4, :, 0:64], pf[bb][0:32, 0:256].rearrange("p (g d) -> p g d", g=4))
                    OP("dve", "memset", [], [("v1", 0)], vsb[:, :, :, 64:65], 1.0)
                    if pas == 0 and npass == 2:
                        OP("dve", "tensor_copy", keys("kT", [0, 1], [7]), ["kst"], kst[:, l, :, :], kT[:, :, 7 * 128:8 * 128])
                        OP("dve", "tensor_copy", [("v", 7)], ["vst"], vst[:, l, :, 0:64], vsb[:, 7, :, 0:64])
                    stage(7)

                    steps = []

                    def mk_prompt_tile(ti):
                        cols = slice(ti * 128, (ti + 1) * 128)
                        par = ti % 2
                        hasA = (ti > 0) or (pas == 1)
                        for g in range(4):
                            st_ = {}

                            def S(g=g, st_=st_):
                                kc_ = g // 2
                                ph = (g % 2) * 64
                                sp_ = stepc[0] % 3
                                stepc[0] += 1
                                st_["sp"] = sp_
                                if hasA:
                                    if ti > 0:
                                        kA = kT[ph:ph + 64, kc_, (ti - 1) * 128:ti * 128]
                                        rA = [("kT", kc_, ti - 1)]
                                    else:
                                        kA = kst[ph:ph + 64, l, kc_, :]
                                        rA = ["kst"]
                                for hp in range(2):
                                    Sb = nbS()
                                    pk = ("PT", sp_, hp)
                                    PTh = PT[:, sp_, hp * 512:(hp + 1) * 512]
                                    for hh in range(2):
                                        h = 4 * g + 2 * hp + hh
                                        cq_ = (h // 8) * 4 + (h % 4)
                                        qap = qa[ph:ph + 64, cq_, cols]
                                        if hasA:
                                            MM(Sb, pf[Sb][:, hh * 256:hh * 256 + 128], kA, qap, True, True, rA + [("qa", cq_, ti)])
                                        MM(Sb, pf[Sb][:, hh * 256 + 128:hh * 256 + 256], kT[ph:ph + 64, kc_, cols], qap, True, True, [("kT", kc_, ti), ("qa", cq_, ti)])
                                    for hh in range(2):
                                        h = 4 * g + 2 * hp + hh
                                        lo = hh * 256 + (0 if hasA else 128)
                                        hi = hh * 256 + 256
                                        ndv = ndP[:].rearrange("p a b -> p (a b)")[:, (0 if hasA else 128):256]
                                        OP("dve", "scalar_tensor_tensor", ["ndP"], [("ps", Sb)], out=pf[Sb][:, lo:hi], in0=ndv, scalar=SLOPES[h], in1=pf[Sb][:, lo:hi], op0=ALU.mult, op1=ALU.add)
                                        if not hasA:
                                            OP("act", "activation", [], [("ps", Sb), pk + (hh, 0), pk + (hh, 1)], out=PTh[:, lo:hi], in_=pf[Sb][:, lo:hi], func=AF.Exp)
                                    if hasA:
                                        OP("act", "activation", [], [("ps", Sb), pk + (0, 0), pk + (0, 1), pk + (1, 0), pk + (1, 1)], out=PTh[:, 0:512], in_=pf[Sb][:, 0:512], func=AF.Exp)

                            def V(g=g, st_=st_):
                                sp_ = st_["sp"]
                                if hasA:
                                    if ti > 0:
                                        vA = vsb[:, ti - 1, g, :]
                                        rvA = [("v", ti - 1), ("v1", 0)]
                                    else:
                                        vA = vst[:, l, g, :]
                                        rvA = ["vst"]
                                Ob = nbO()
                                for hp in range(2):
                                    pk = ("PT", sp_, hp)
                                    PTh = PT[:, sp_, hp * 512:(hp + 1) * 512]
                                    for hh in range(2):
                                        hs = 2 * hp + hh
                                        oo = pf[Ob][:, hs * 65:(hs + 1) * 65]
                                        if hasA:
                                            MM(Ob, oo, PTh[:, hh * 256:hh * 256 + 128], vA, True, False, [pk + (hh, 0)] + rvA)
                                        MM(Ob, oo, PTh[:, hh * 256 + 128:hh * 256 + 256], vsb[:, ti, g, :], (not hasA), True, [pk + (hh, 1), ("v", ti), ("v1", 0)])
                                O = pf[Ob][:, 0:260].rearrange("p (h d) -> p h d", h=4)
                                dn = den[:, g % 2, :]
                                dk = ("den", g % 2)
                                OP("dve", "tensor_tensor", ["esink"], [("ps", Ob), dk], out=dn, in0=O[:, :, 64], in1=esink[:, l * 16 + 4 * g:l * 16 + 4 * g + 4], op=ALU.add)
                                OP("dve", "reciprocal", [dk], [dk], dn, dn)
                                OP("dve", "tensor_tensor", [dk], [("ps", Ob), ("atok", par)], out=atok[:, par, g * 256:(g + 1) * 256].rearrange("p (h d) -> p h d", h=4),
                                   in0=O[:, :, 0:64], in1=dn.unsqueeze(2).to_broadcast([128, 4, 64]), op=ALU.mult)

                            steps.append((S, V))

                        def TAIL():
                            bt = nbt()
                            for c in range(8):
                                TRB(bt, pbt[bt][:, c * 128:(c + 1) * 128], atok[:, par, c * 128:(c + 1) * 128], identb[:], [("atok", par), "identb"])
                            OP("act", "activation", [], [("pt", bt)] + keys("qa", range(8), [ti]), out=qa[:, :, cols], in_=pbt[bt][:].rearrange("p (c t) -> p c t", c=8), func=AF.Copy)

                        steps.append((None, TAIL))

                    def mk_sample_tile(ti):
                        cols = slice(ti * 128, (ti + 1) * 128)
                        for g in range(4):
                            st_ = {}

                            def S(g=g, st_=st_):
                                kc_ = g // 2
                                ph = (g % 2) * 64
                                sp_ = stepc[0] % 3
                                stepc[0] += 1
                                st_["sp"] = sp_
                                for hp in range(2):
                                    Sb = nbS()
                                    pk = ("PT", sp_, hp)
                                    PTh = PT[:, sp_, hp * 512:(hp + 1) * 512]
                                    for hh in range(2):
                                        h = 4 * g + 2 * hp + hh
                                        cq_ = (h // 8) * 4 + (h % 4)
                                        for b4 in range(4):
                                            qap = qa[ph:ph + 64, cq_, ti * 128 + b4 * 32:ti * 128 + (b4 + 1) * 32]
                                            MM(Sb, pf[Sb][:, hh * 256 + b4 * 32:hh * 256 + (b4 + 1) * 32], kcT[ph:ph + 64, b4, kc_, :], qap, True, True, [("kcT", b4), ("qa", cq_, ti)])
                                            MM(Sb, pf[Sb][0:32, hh * 256 + 128 + b4 * 32:hh * 256 + 128 + (b4 + 1) * 32],
                                               kT[ph:ph + 64, kc_, ti * 128 + b4 * 32:ti * 128 + (b4 + 1) * 32], qap, True, True, [("kT", kc_, ti), ("qa", cq_, ti)])
                                    for hh in range(2):
                                        h = 4 * g + 2 * hp + hh
                                        lo = hh * 256
                                        OP("dve", "scalar_tensor_tensor", ["ndS"], [("ps", Sb)], out=pf[Sb][:, lo:lo + 128], in0=ndS[:, 0, :], scalar=SLOPES[h], in1=pf[Sb][:, lo:lo + 128], op0=ALU.mult, op1=ALU.add)
                                        OP("dve", "scalar_tensor_tensor", ["ndS"], [("ps", Sb)], out=pf[Sb][0:32, lo + 128:lo + 256], in0=ndS[0:32, 1, :], scalar=SLOPES[h], in1=pf[Sb][0:32, lo + 128:lo + 256], op0=ALU.mult, op1=ALU.add)
                                        OP("act", "activation", [], [("ps", Sb), pk + (hh, 0)], out=PTh[:, lo:lo + 128], in_=pf[Sb][:, lo:lo + 128], func=AF.Exp)
                                        OP("act", "activation", [], [("ps", Sb), pk + (hh, 1)], out=PTh[0:32, lo + 128:lo + 256], in_=pf[Sb][0:32, lo + 128:lo + 256], func=AF.Exp)

                            def V(g=g, st_=st_):
                                sp_ = st_["sp"]
                                for b4 in range(4):
                                    Ob = nbO()
                                    for hs in range(4):
                                        hp, hh = hs // 2, hs % 2
                                        PTh = PT[:, sp_, hp * 512:(hp + 1) * 512]
                                        oo = pf[Ob][0:32, hs * 65:(hs + 1) * 65]
                                        MM(Ob, oo, PTh[:, hh * 256 + b4 * 32:hh * 256 + (b4 + 1) * 32], vc[:, b4, g, :], True, False, [("PT", sp_, hp, hh, 0), ("vc", b4)])
                                        MM(Ob, oo, PTh[0:32, hh * 256 + 128 + b4 * 32:hh * 256 + 128 + (b4 + 1) * 32], vnew[0:32, b4, g, :], False, True, [("PT", sp_, hp, hh, 1), ("vnew", b4)])
                                    O = pf[Ob][0:32, 0:260].rearrange("p (h d) -> p h d", h=4)
                                    dn = den[0:32, b4 % 2, :]
                                    dk = ("den", b4 % 2)
                                    OP("dve", "tensor_tensor", ["esink"], [("ps", Ob), dk], out=dn, in0=O[:, :, 64], in1=esink[0:32, l * 16 + 4 * g:l * 16 + 4 * g + 4], op=ALU.add)
                                    OP("dve", "reciprocal", [dk], [dk], dn, dn)
                                    OP("dve", "tensor_tensor", [dk], [("ps", Ob), "atoks"], out=atoks[:, b4 * 1024 + g * 256:b4 * 1024 + (g + 1) * 256].rearrange("p (h d) -> p h d", h=4),
                                       in0=O[:, :, 0:64], in1=dn.unsqueeze(2).to_broadcast([32, 4, 64]), op=ALU.mult)

                            steps.append((S, V))

                        def TAIL():
                            bt = nbt()
                            for b4 in range(4):
                                for c in range(8):
                                    TRB(bt, pbt[bt][:, c * 128 + b4 * 32:c * 128 + (b4 + 1) * 32], atoks[:, b4 * 1024 + c * 128:b4 * 1024 + (c + 1) * 128], identb[0:32, 0:32], ["atoks", "identb"])
                            OP("act", "activation", [], [("pt", bt)] + keys("qa", range(8), [ti]), out=qa[:, :, cols], in_=pbt[bt][:].rearrange("p (c t) -> p c t", c=8), func=AF.Copy)

                        steps.append((None, TAIL))

                    for ti, (kind, gi_) in enumerate(tiles):
                        if kind == "s":
                            mk_sample_tile(ti)
                    for ti, (kind, gi_) in enumerate(tiles):
                        if kind == "p":
                            mk_prompt_tile(ti)
                    pipe = {"i": 0, "fly": [None, None]}

                    def tick():
                        nxt = None
                        if pipe["i"] < len(steps):
                            nxt = steps[pipe["i"]]
                            pipe["i"] += 1
                        if nxt is not None and nxt[0] is not None:
                            nxt[0]()
                        old_ = pipe["fly"].pop(0)
                        if old_ is not None:
                            old_[1]()
                        pipe["fly"].append(nxt)

                    def flush():
                        while pipe["i"] < len(steps) or any(f is not None for f in pipe["fly"]):
                            tick()

                    n_sample_steps = 5 if has_s else 0

                    bmode[0] = "all"
                    for fg in range(2):
                        Wch, kch = next_unit("ch")
                        Wcc, kcc = next_unit("cc", 2)
                        Wcb, kcb = next_unit("cb", 3)
                        for jj in range(4):
                            j = fg * 4 + jj
                            if pas == 1:
                                OP("dve", "tensor_copy", ["ust"], ["u"], ubuf[:, 0:2], ust[:, l, j, :])
                            if has_s:
                                OP("dve", "tensor_copy", ["scT"], ["us"], usb[:, :, 0:2], scT[:, j, :].rearrange("p (b r) -> p b r", b=4))
                            for gidx, (n0, n1) in enumerate(groups):
                                N = n1 - n0
                                tl = list(gt(n0, n1))
                                is_s = has_s and gidx == 2
                                b1 = nb()
                                for kc in range(8):
                                    MM(b1, pf[b1][:, 0:N], Wch[:, kc, jj * 128:(jj + 1) * 128], hb[:, kc, n0:n1], kc == 0, kc == 7, keys("h", [kc], tl) + [kch])
                                OP("act", "activation", [], [("ps", b1), "tB"], out=tB[:, 0:N], in_=pf[b1][:, 0:N], func=AF.Copy)
                                b2 = nb()
                                for kc in range(8):
                                    MM(b2, pf[b2][:, 0:N], Wcc[:, kc, jj * 128:(jj + 1) * 128], hb[:, kc, n0:n1], kc == 0, kc == 7, keys("h", [kc], tl) + [kcc])
                                if not is_s:
                                    OP("dve", "tensor_tensor", ["tB"], [("ps", b2), "u"], out=ubuf[:, 2 + n0:2 + n1], in0=pf[b2][:, 0:N], in1=tB[:, 0:N], op=ALU.mult)
                                    u0, u1, u2 = ubuf[:, n0:n1], ubuf[:, n0 + 1:n1 + 1], ubuf[:, n0 + 2:n1 + 2]
                                    cv = tC[:, 0:N]
                                    ukey = "u"
                                else:
                                    OP("dve", "tensor_tensor", ["tB"], [("ps", b2), "us"], out=usb[:, :, 2:34], in0=pf[b2][:, 0:N].rearrange("p (b i) -> p b i", b=4),
                                       in1=tB[:, 0:N].rearrange("p (b i) -> p b i", b=4), op=ALU.mult)
                                    u0, u1, u2 = usb[:, :, 0:32], usb[:, :, 1:33], usb[:, :, 2:34]
                                    cv = tC[:, 0:N].rearrange("p (b i) -> p b i", b=4)
                                    ukey = "us"
                                w0 = gT[:, j, 17 + l * 3 + 0:17 + l * 3 + 1]
                                w1 = gT[:, j, 17 + l * 3 + 1:17 + l * 3 + 2]
                                w2 = gT[:, j, 17 + l * 3 + 2:17 + l * 3 + 3]
                                OP("dve", "tensor_scalar", [ukey, "gT"], ["tC"], out=cv, in0=u0, scalar1=w0, scalar2=None, op0=ALU.mult)
                                OP("dve", "scalar_tensor_tensor", [ukey, "tC", "gT"], ["tC"], out=cv, in0=u1, scalar=w1, in1=cv, op0=ALU.mult, op1=ALU.add)
                                OP("dve", "scalar_tensor_tensor", [ukey, "tC", "gT"], ["tC"], out=cv, in0=u2, scalar=w2, in1=cv, op0=ALU.mult, op1=ALU.add)
                                b3 = nb()
                                for kc in range(8):
                                    MM(b3, pf[b3][:, 0:N], Wcb[:, kc, jj * 128:(jj + 1) * 128], hb[:, kc, n0:n1], kc == 0, kc == 7, keys("h", [kc], tl) + [kcb])
                                OP("dve", "tensor_tensor", ["tC"], [("ps", b3)] + keys("cb", [j], tl), out=cbc[:, j, n0:n1], in0=pf[b3][:, 0:N], in1=tC[:, 0:N], op=ALU.mult)
                                tick()
                            if pas == 0:
                                OP("dve", "tensor_copy", ["u"], ["ust"], ust[:, l, j, :], ubuf[:, 1024:1026])
                                OP("dve", "tensor_copy", ["us"], ["cvoS"], cvoS[:, j, :].rearrange("p (b r) -> p b r", b=4), usb[:, :, 32:34])
                            else:
                                OP("dve", "tensor_copy", ["u"], ["cvoP"], cvoP[:, j, :], ubuf[:, 1024:1026])
                    stage(5)
                    if pas == 0:
                        nr, cvo, dst = 8, cvoS, convs[l]
                    else:
                        nr, cvo, dst = 2, cvoP, convp[l]
                    if pas == 0 or npass == 2:
                        for half in range(2):
                            b = nb()
                            for i in range(4):
                                j = half * 4 + i
                                TRF(b, pf[b][0:nr, i * 128:(i + 1) * 128], cvo[:, j, :], ident[:], ["cvoS" if pas == 0 else "cvoP", "ident"])
                            OP("act", "activation", [], [("ps", b), "xtok0", "xtok1"], out=xtok[0:nr, half * 512:(half + 1) * 512], in_=pf[b][0:nr, :], func=AF.Copy)
                        DMA("sp", dst, xtok[0:nr, :], ["xtok0", "xtok1"], [])

                    if has_s:
                        assert pipe["i"] > n_sample_steps + 2
                    stage(6)
                    bmode[0] = "split"
                    for hf in range(2):
                        Wg, kg = next_unit("gb")
                        Wo, ko = next_unit("wco", 2)
                        for jj in range(4):
                            j = hf * 4 + jj
                            for (n0, n1) in groups:
                                N = n1 - n0
                                tl = list(gt(n0, n1))
                                bg = nb()
                                for kc in range(8):
                                    MM(bg, pf[bg][:, 0:N], Wg[:, kc, jj * 128:(jj + 1) * 128], hb[:, kc, n0:n1], kc == 0, kc == 7, keys("h", [kc], tl) + [kg])
                                OP("act", "activation", [], [("ps", bg), "tB"], out=tB[:, 0:N], in_=pf[bg][:, 0:N], func=AF.Sigmoid)
                                bo = nb()
                                for kc in range(8):
                                    MM(bo, pf[bo][:, 0:N], Wo[:, kc, jj * 128:(jj + 1) * 128], cbc[:, kc, n0:n1], kc == 0, kc == 7, keys("cb", [kc], tl) + [ko])
                                OP("dve", "tensor_tensor", ["tB"], [("ps", bo)] + keys("m", [j], tl), out=m_[:, j, n0:n1], in0=pf[bo][:, 0:N], in1=tB[:, 0:N], op=ALU.mult)
                                tick()
                    flush()
                    bmode[0] = "all"
                    stage(8)
                    stage(9)
                    for hf in range(2):
                        Wg, kg = next_unit("ga")
                        Wo, ko = next_unit("wao", 2)
                        for jj in range(4):
                            j = hf * 4 + jj
                            for (n0, n1) in groups:
                                N = n1 - n0
                                tl = list(gt(n0, n1))
                                bg = nb()
                                for kc in range(8):
                                    MM(bg, pf[bg][:, 0:N], Wg[:, kc, jj * 128:(jj + 1) * 128], hb[:, kc, n0:n1], kc == 0, kc == 7, keys("h", [kc], tl) + [kg])
                                OP("act", "activation", [], [("ps", bg), "tB"], out=tB[:, 0:N], in_=pf[bg][:, 0:N], func=AF.Sigmoid)
                                bo = nb()
                                for kc in range(8):
                                    MM(bo, pf[bo][:, 0:N], Wo[:, kc, jj * 128:(jj + 1) * 128], qa[:, kc, n0:n1], kc == 0, kc == 7, keys("qa", [kc], tl) + [ko])
                                OP("dve", "tensor_tensor", ["tB"], [("ps", bo), "tC"], out=tC[:, 0:N], in0=pf[bo][:, 0:N], in1=tB[:, 0:N], op=ALU.mult)
                                OP("dve", "tensor_tensor", ["tC"] + keys("m", [j], tl), keys("m", [j], tl), out=m_[:, j, n0:n1], in0=m_[:, j, n0:n1], in1=tC[:, 0:N], op=ALU.add)
                    P.barrier()
                    stage(10)
                    if has_s:
                        for b4 in range(4):
                            P.op("pool", lambda e, l=l, b4=b4: e.dma_start(out=tinM[:, b4, :, :], in_=cmk[l, b4].rearrange("(k p) f -> p k f", p=128)),
                                 writes=[("tinM", b4)], dma=True, dsem=ds("pool"))
                            for kt in range(2):
                                P.op("pool", lambda e, l=l, b4=b4, kt=kt: e.dma_start(out=mvs[:, b4, kt, :, 0:64], in_=cmv[l, b4, kt * 128:(kt + 1) * 128, :].rearrange("p (h d) -> p h d", h=4)),
                                     writes=[("mvs", b4)], dma=True, dsem=ds("pool"))
                        OP("dve", "memset", [], [("mvs1", 0)], mvs[:, :, :, :, 64:65].rearrange("p b k h d -> p (b k h) d"), 1.0)

                    def xPREP(gi):
                        for b4 in range(4):
                            bt = nbt()
                            for c in range(2):
                                for kt in range(2):
                                    TRB(bt, pbt[bt][:, (c * 2 + kt) * 128:(c * 2 + kt + 1) * 128], tinM[:, b4, kt, c * 128:(c + 1) * 128], identb[:], [("tinM", b4), "identb"])
                            OP("act", "activation", [], [("pt", bt), ("mkTs", b4)], out=mkTs[:, b4, :, :].rearrange("p c k -> p (c k)"), in_=pbt[bt][:, 0:512], func=AF.Copy)

                    stage(101)
                    Wm0, km0 = next_unit("wmix")
                    Wm1, km1 = next_unit("wmix", 2)
                    Wq, kq = next_unit("wcq", 3)
                    Wx, kx = next_unit("wcox", 4)
                    PTxb = [PTx, PTx2]

                    def xMIX(gi):
                        n0, n1 = groups[gi]
                        N = n1 - n0
                        tl = list(gt(n0, n1))
                        for j in range(8):
                            Wm, km = (Wm0, km0) if j < 4 else (Wm1, km1)
                            jj = j % 4
                            b = nb()
                            for kc in range(8):
                                MM(b, pf[b][:, 0:N], Wm[:, kc, jj * 128:(jj + 1) * 128], m_[:, kc, n0:n1], kc == 0, kc == 7, keys("m", [kc], tl) + [km])
                            OP("dve", "tensor_tensor", keys("x", [j], tl), [("ps", b)] + keys("x", [j], tl), out=xT[:, j, n0:n1], in0=xT[:, j, n0:n1], in1=pf[b][:, 0:N], op=ALU.add)
                        if gi == len(groups) - 1:
                            release(2)

                    def xRMS(gi):
                        rms(4 + l, [groups[gi]])

                    def xCQ(gi):
                        n0, n1 = groups[gi]
                        N = n1 - n0
                        tl = list(gt(n0, n1))
                        for c in range(2):
                            b = nb()
                            for kc in range(8):
                                MM(b, pf[b][:, 0:N], Wq[:, kc, c * 128:(c + 1) * 128], hb[:, kc, n0:n1], kc == 0, kc == 7, keys("h", [kc], tl) + [kq])
                            OP("act", "activation", [], [("ps", b)] + keys("cq", [c], tl), out=cqT[:, c, n0:n1], in_=pf[b][:, 0:N], func=AF.Copy, scale=0.125)

                    def xS(gi):
                        n0, n1 = groups[gi]
                        N = n1 - n0
                        tl = list(gt(n0, n1))
                        PX = PTxb[gi % 2]
                        is_s = has_s and gi == 2
                        if not is_s:
                            for hm in range(4):
                                c = hm // 2
                                ph = (hm % 2) * 64
                                for kt in range(2):
                                    b = nb()
                                    MM(b, pf[b][:, 0:N], mkTp[ph:ph + 64, l, c, kt * 128:(kt + 1) * 128], cqT[ph:ph + 64, c, n0:n1], True, True, ["mkTp"] + keys("cq", [c], tl))
                                    OP("act", "activation", [], [("ps", b), ("PTx", gi % 2, hm * 2 + kt)], out=PX[:, hm * 2 + kt, 0:N], in_=pf[b][:, 0:N], func=AF.Exp)
                        else:
                            for hm in range(4):
                                c = hm // 2
                                ph = (hm % 2) * 64
                                for kt in range(2):
                                    for b4 in range(4):
                                        b = nb()
                                        MM(b, pf[b][:, 0:128], mkTs[ph:ph + 64, b4, c, kt * 128:(kt + 1) * 128], cqT[ph:ph + 64, c, n0:n1], True, True,
                                           [("mkTs", b4)] + keys("cq", [c], tl))
                                        OP("act", "activation", [], [("ps", b), ("PTx", gi % 2, hm * 2 + kt)], out=PX[:, hm * 2 + kt, b4 * 32:(b4 + 1) * 32],
                                           in_=pf[b][:, b4 * 32:(b4 + 1) * 32], func=AF.Exp)

                    def xPV(gi):
                        n0, n1 = groups[gi]
                        tl = list(gt(n0, n1))
                        PX = PTxb[gi % 2]
                        is_s = has_s and gi == 2
                        if not is_s:
                            for ti in tl:
                                par = ti % 2
                                kk = (ti // 2) % 4
                                off = ti * 128 - n0
                                Ob = nb()
                                for hm in range(4):
                                    for kt in range(2):
                                        MM(Ob, pf[Ob][:, hm * 65:(hm + 1) * 65], PX[:, hm * 2 + kt, off:off + 128], mvp_sb[:, l, kt, hm, :], kt == 0, kt == 1, [("PTx", gi % 2, hm * 2 + kt), "mvp_sb"])
                                O = pf[Ob][:, 0:260].rearrange("p (h d) -> p h d", h=4)
                                dn = den[:, par, :]
                                dk = ("den", par)
                                OP("dve", "reciprocal", [], [("ps", Ob), dk], dn, O[:, :, 64])
                                OP("dve", "tensor_tensor", [dk], [("ps", Ob), ("atokx", par, kk)], out=atok[:, par, kk * 256:(kk + 1) * 256].rearrange("p (h d) -> p h d", h=4),
                                   in0=O[:, :, 0:64], in1=dn.unsqueeze(2).to_broadcast([128, 4, 64]), op=ALU.mult)
                        else:
                            for b4 in range(4):
                                Ob = nb()
                                for hm in range(4):
                                    for kt in range(2):
                                        MM(Ob, pf[Ob][0:32, hm * 65:(hm + 1) * 65], PX[:, hm * 2 + kt, b4 * 32:(b4 + 1) * 32], mvs[:, b4, kt, hm, :], kt == 0, kt == 1,
                                           [("PTx", gi % 2, hm * 2 + kt), ("mvs", b4), ("mvs1", 0)])
                                O = pf[Ob][0:32, 0:260].rearrange("p (h d) -> p h d", h=4)
                                dn = den[0:32, b4 % 2, :]
                                dk = ("den", b4 % 2)
                                OP("dve", "reciprocal", [], [("ps", Ob), dk], dn, O[:, :, 64])
                                OP("dve", "tensor_tensor", [dk], [("ps", Ob), ("atokx", 0, b4)], out=atok[0:32, 0, b4 * 256:(b4 + 1) * 256].rearrange("p (h d) -> p h d", h=4),
                                   in0=O[:, :, 0:64], in1=dn.unsqueeze(2).to_broadcast([32, 4, 64]), op=ALU.mult)

                    def xTR(gi):
                        n0, n1 = groups[gi]
                        tl = list(gt(n0, n1))
                        is_s = has_s and gi == 2
                        if not is_s:
                            for ti in tl:
                                par = ti % 2
                                kk = (ti // 2) % 4
                                bt = nbt()
                                for c in range(2):
                                    TRB(bt, pbt[bt][:, c * 128:(c + 1) * 128], atok[:, par, kk * 256 + c * 128:kk * 256 + (c + 1) * 128], identb[:], [("atokx", par, kk), "identb"])
                                OP("act", "activation", [], [("pt", bt)] + keys("co", [0, 1], [ti]), out=coT[:, :, ti * 128:(ti + 1) * 128], in_=pbt[bt][:, 0:256].rearrange("p (c t) -> p c t", c=2), func=AF.Copy)
                        else:
                            ti = s_ti
                            bt = nbt()
                            for b4 in range(4):
                                for c in range(2):
                                    TRB(bt, pbt[bt][:, c * 128 + b4 * 32:c * 128 + (b4 + 1) * 32], atok[0:32, 0, b4 * 256 + c * 128:b4 * 256 + (c + 1) * 128], identb[0:32, 0:32], [("atokx", 0, b4), "identb"])
                            OP("act", "activation", [], [("pt", bt)] + keys("co", [0, 1], [ti]), out=coT[:, :, ti * 128:(ti + 1) * 128], in_=pbt[bt][:, 0:256].rearrange("p (c t) -> p c t", c=2), func=AF.Copy)

                    def xWX(gi):
                        n0, n1 = groups[gi]
                        N = n1 - n0
                        tl = list(gt(n0, n1))
                        for j in range(8):
                            b = nb()
                            for kc in range(2):
                                MM(b, pf[b][:, 0:N], Wx[:, kc, j * 128:(j + 1) * 128], coT[:, kc, n0:n1], kc == 0, kc == 1, keys("co", [kc], tl) + [kx])
                            OP("dve", "tensor_tensor", keys("x", [j], tl), [("ps", b)] + keys("x", [j], tl), out=xT[:, j, n0:n1], in0=xT[:, j, n0:n1], in1=pf[b][:, 0:N], op=ALU.add)
                        rms(8 + l, [(n0, n1)])

                    if len(groups) == 3:
                        seq = [(xMIX, 0), (xMIX, 1), (xRMS, 0), (xMIX, 2), (xRMS, 1), (xCQ, 0), (xS, 0), (xRMS, 2), (xCQ, 1), (xS, 1), (xPREP, 0), (xPV, 0), (xCQ, 2), (xS, 2),
                               (xTR, 0), (xPV, 1), (xWX, 0), (xTR, 1), (xPV, 2), (xWX, 1), (xTR, 2), (xWX, 2)]
                    else:
                        seq = [(xMIX, 0), (xMIX, 1), (xRMS, 0), (xRMS, 1), (xCQ, 0), (xS, 0), (xCQ, 1), (xS, 1), (xPV, 0), (xTR, 0), (xPV, 1), (xWX, 0), (xTR, 1), (xWX, 1)]
                    for fn_, gi_x in seq:
                        fn_(gi_x)
                    stage(104)
                    P.barrier()

                    stage(11)
                    for hf in range(2):
                        tg = 0
                        for uu in range(4):
                            Wu, ku = next_unit("up")
                            for jj in range(4):
                                ci = uu * 4 + jj
                                for (n0, n1) in groups:
                                    N = n1 - n0
                                    tl = list(gt(n0, n1))
                                    b = nb()
                                    for kc in range(8):
                                        MM(b, pf[b][:, 0:N], Wu[:, kc, jj * 128:(jj + 1) * 128], hb[:, kc, n0:n1], kc == 0, kc == 7, keys("h", [kc], tl) + [ku])
                                    tg += 1
                                    tt, tk = (tB, "tB") if tg % 2 else (tC, "tC")
                                    OP("act", "activation", [], [("ps", b), tk], out=tt[:, 0:N], in_=pf[b][:, 0:N], func=AF.Relu)
                                    OP("dve", "tensor_tensor", [tk], keys("hid", [ci], tl), out=hid[:, ci, n0:n1], in0=tt[:, 0:N], in1=tt[:, 0:N], op=ALU.mult)
                        for j in range(8 if hf == 0 else 6):
                            Wd, kd = next_unit("down")
                            for (n0, n1) in groups:
                                N = n1 - n0
                                tl = list(gt(n0, n1))
                                b = nb()
                                for kc in range(16):
                                    MM(b, pf[b][:, 0:N], Wd[:, kc, :], hid[:, kc, n0:n1], kc == 0, kc == 15, keys("hid", [kc], tl) + [kd])
                                OP("dve", "tensor_tensor", keys("x", [j], tl), [("ps", b)] + keys("x", [j], tl), out=xT[:, j, n0:n1], in0=xT[:, j, n0:n1], in1=pf[b][:, 0:N], op=ALU.add)
                        if hf == 1:
                            Wd6, kd6 = next_unit("down")
                            Wd7, kd7 = next_unit("down", 2)
                            for (n0, n1) in groups:
                                N = n1 - n0
                                tl = list(gt(n0, n1))
                                for (Wd, kd, j) in ((Wd6, kd6, 6), (Wd7, kd7, 7)):
                                    b = nb()
                                    for kc in range(16):
                                        MM(b, pf[b][:, 0:N], Wd[:, kc, :], hid[:, kc, n0:n1], kc == 0, kc == 15, keys("hid", [kc], tl) + [kd])
                                    OP("dve", "tensor_tensor", keys("x", [j], tl), [("ps", b)] + keys("x", [j], tl), out=xT[:, j, n0:n1], in0=xT[:, j, n0:n1], in1=pf[b][:, 0:N], op=ALU.add)
                                if l < nlayers - 1:
                                    rms(l + 1, [(n0, n1)])
                    P.barrier()

                stage(12)
                for (n0, n1) in groups:
                    N = n1 - n0
                    tl = list(gt(n0, n1))
                    b = nb()
                    for c in range(8):
                        s2 = c % 2
                        OP("act", "activation", keys("x", [c], tl), [("sq", s2)], out=sq[:, s2, 0:N], in_=xT[:, c, n0:n1], func=AF.Square)
                        MM(b, pf[b][:, 0:N], onesb[:], sq[:, s2, 0:N], c == 0, c == 7, [("sq", s2), "onesb"])
                    OP("act", "activation", ["epsT"], [("ps", b), "tA"], out=tA[:, 0:N], in_=pf[b][:, 0:N], func=AF.Ln, scale=1.0 / 1024.0, bias=epsT[:])
                    OP("act", "activation", ["tA"], ["tA"], out=tA[:, 0:N], in_=tA[:, 0:N], func=AF.Exp, scale=-0.5)
                    for ti in tl:
                        kind, gi_ = tiles[ti]
                        off = ti * 128 - n0
                        dst = yp[gi_ * 128:(gi_ + 1) * 128, :] if kind == "p" else ys
                        for half in range(2):
                            tt, tk = (tB, "tB") if half == 0 else (tC, "tC")
                            hk = "xtok%d" % half
                            for i in range(4):
                                c = half * 4 + i
                                OP("dve", "scalar_tensor_tensor", keys("x", [c], [ti]) + ["tA", "gT"], [tk],
                                   out=tt[:, i * 128:(i + 1) * 128], in0=xT[:, c, ti * 128:(ti + 1) * 128], scalar=gT[:, c, 16:17], in1=tA[:, off:off + 128], op0=ALU.mult, op1=ALU.mult)
                            b2 = nb()
                            for i in range(4):
                                TRF(b2, pf[b2][:, i * 128:(i + 1) * 128], tt[:, i * 128:(i + 1) * 128], ident[:], [tk, "ident"])
                            OP("act", "activation", [], [("ps", b2), hk], out=xtok[:, half * 512:(half + 1) * 512], in_=pf[b2][:], func=AF.Copy)
                            DMA("sp", dst[:, half * 512:(half + 1) * 512], xtok[:, half * 512:(half + 1) * 512], [hk], [])
                P.barrier()

        try:
            _body()
        except _Stop:
            pass
        if _STOP == 0:
            assert wstate["used"] == len(units), (wstate["used"], len(units))
        P.wait_all_dma("sp")
        stats = P.emit(esems)
    return nc, stats


_CACHE = {}


def kernel(x_prompt, x_sample, mem_prompt, cache_attn_k, cache_attn_v, state_conv, cache_mem_k, cache_mem_v,
           norm_mix_g, w_in, conv_w, attn_sink, w_attn_out, w_conv_out, w_mix_out, norm_cross_g, norm_mem_g,
           w_cq, w_ckv, w_co, norm_mlp_g, w_up, w_down, norm_final_g):
    f = lambda a: np.ascontiguousarray(np.asarray(a, dtype=np.float32))
    x_prompt, x_sample, mem_prompt = f(x_prompt), f(x_sample), f(mem_prompt)
    cache_attn_k, cache_attn_v, state_conv = f(cache_attn_k), f(cache_attn_v), f(state_conv)
    cache_mem_k, cache_mem_v = f(cache_mem_k), f(cache_mem_v)
    w_in = f(w_in)
    qcols = np.concatenate([np.arange(h * 64, (h + 1) * 64) for h in QPERM_HEADS])
    perm = np.concatenate([qcols, np.arange(1024, 6656)])
    w_in_p = np.ascontiguousarray(w_in[:, :, perm])
    gall = np.ascontiguousarray(np.concatenate([f(norm_mix_g), f(norm_cross_g), f(norm_mlp_g), f(norm_mem_g),
                                                f(norm_final_g)[None, :], f(conv_w).reshape(12, 1024)], axis=0))
    if "nc" not in _CACHE:
        _CACHE["nc"] = build()[0]
    nc = _CACHE["nc"]
    shared = {
        "gall": gall, "sink": f(attn_sink), "w_in": w_in_p, "wao": f(w_attn_out), "wco": f(w_conv_out),
        "wmix": f(w_mix_out), "wcq": f(w_cq), "wckv": f(w_ckv), "wcox": f(w_co), "wup": f(w_up), "wdown": f(w_down),
    }
    in_maps = []
    for i in range(8):
        sl = slice(4 * i, 4 * i + 4)
        d = dict(shared)
        d["xp"] = x_prompt[i]
        d["xs"] = np.ascontiguousarray(x_sample[sl].reshape(128, 1024))
        d["mem"] = mem_prompt[i]
        d["cak"] = np.ascontiguousarray(cache_attn_k[:, sl].reshape(4, 4, 128, 256))
        d["cav"] = np.ascontiguousarray(cache_attn_v[:, sl].reshape(4, 4, 128, 256))
        d["sconv"] = np.ascontiguousarray(state_conv[:, sl].reshape(4, 8, 1024))
        d["cmk"] = np.ascontiguousarray(cache_mem_k[:, sl].reshape(4, 4, 256, 256))
        d["cmv"] = np.ascontiguousarray(cache_mem_v[:, sl].reshape(4, 4, 256, 256))
        in_maps.append(d)
    res = run_bass_kernel_spmd(nc, in_maps, core_ids=list(range(8)))
    R = res.results
    y_prompt = np.stack([R[i]["yp"] for i in range(8)], 0)
    y_sample = np.concatenate([R[i]["ys"].reshape(4, 32, 1024) for i in range(8)], 0)
    kpo = np.stack([R[i]["kp"].reshape(4, 128, 4, 64) for i in range(8)], 1)
    vpo = np.stack([R[i]["vp"].reshape(4, 128, 4, 64) for i in range(8)], 1)
    cpo = np.stack([R[i]["convp"] for i in range(8)], 1)
    mko = np.stack([R[i]["mkp"].reshape(4, 256, 4, 64) for i in range(8)], 1)
    mvo = np.stack([R[i]["mvp"].reshape(4, 256, 4, 64) for i in range(8)], 1)
    kso = np.concatenate([R[i]["ksn"].reshape(4, 4, 32, 4, 64) for i in range(8)], 1)
    vso = np.concatenate([R[i]["vsn"].reshape(4, 4, 32, 4, 64) for i in range(8)], 1)
    cso = np.concatenate([R[i]["convs"].reshape(4, 4, 2, 1024) for i in range(8)], 1)
    out = (y_prompt, y_sample, kpo, vpo, cpo, mko, mvo, kso, vso, cso)
    return tuple(np.ascontiguousarray(o.astype(np.float32)) for o in out)
```

```python
import contextlib
import numpy as np
import concourse.bass as bass
import concourse.mybir as mybir
from concourse.bass_utils import run_bass_kernel_spmd

F32 = mybir.dt.float32
BF16 = mybir.dt.bfloat16
ALU = mybir.AluOpType
AF = mybir.ActivationFunctionType

ENGS = ("pe", "act", "dve", "pool", "sp")
COMPUTE = ("pe", "act", "dve", "pool")


class Op:
    __slots__ = ("eng", "fn", "waits", "signal", "semval", "dma", "sem", "clock", "idx", "inc")


class DSem:
    def __init__(self, sem):
        self.sem = sem
        self.count = 0
        self.last = None


class Prog:
    def __init__(self, nc):
        self.nc = nc
        self.ops = {e: [] for e in ENGS}
        self.res = {}
        self.run = {e: {} for e in ENGS}
        self.cnt = {e: 0 for e in ENGS}
        self.ndma = 0
        self.dma_ops = []
        self.lastc = {}

    def _need(self, op, d, raw, hard=True):
        if d is None:
            return
        run = self.run[op.eng]
        if d.dma:
            if run.get(("d", d.idx), 0) >= 1:
                return
        else:
            if d.eng == op.eng and not op.dma:
                if op.eng == "pe" or not hard:
                    return
            if run.get(d.eng, 0) >= d.idx:
                return
        d.signal = True
        op.waits.append(d)
        for k, v in d.clock.items():
            if run.get(k, 0) < v:
                run[k] = v

    def _mk(self, eng, fn, dma):
        o = Op()
        o.eng = eng
        o.fn = fn
        o.waits = []
        o.signal = False
        o.dma = dma
        o.sem = None
        o.semval = None
        o.inc = 1
        return o

    def op(self, eng, fn, reads=(), writes=(), dma=False, dsem=None):
        o = self._mk(eng, fn, dma)
        for r in reads:
            e = self.res.get(r)
            if e is not None:
                self._need(o, e[0], True)
        for w in writes:
            e = self.res.get(w)
            if e is not None:
                hard = not (isinstance(w, tuple) and w[0] in ("ps", "pt"))
                self._need(o, e[0], False, hard)
                for rd in reversed(e[1]):
                    self._need(o, rd, False, hard)
        if dma:
            self._need(o, dsem.last, False)
            self.ndma += 1
            o.idx = self.ndma
            dsem.count += 1
            dsem.last = o
            o.sem = dsem.sem
            o.semval = 16 * dsem.count
            o.inc = 16
            o.signal = True
            o.clock = dict(self.run[eng])
            o.clock[("d", o.idx)] = 1
            self.dma_ops.append(o)
        else:
            self.cnt[eng] += 1
            o.idx = self.cnt[eng]
            o.clock = dict(self.run[eng])
            o.clock[eng] = o.idx
            self.lastc[eng] = o
        for r in reads:
            e = self.res.get(r)
            if e is None:
                self.res[r] = [None, [o]]
            else:
                e[1].append(o)
        for w in writes:
            self.res[w] = [o, []]
        self.ops[eng].append(o)
        return o

    def _pseudo(self, eng, deps):
        o = self._mk(eng, None, False)
        self.cnt[eng] += 1
        o.idx = self.cnt[eng]
        for d in deps:
            self._need(o, d, True)
        o.clock = dict(self.run[eng])
        self.ops[eng].append(o)

    def barrier(self):
        last = dict(self.lastc)
        for e in COMPUTE:
            self._pseudo(e, [d for e2, d in last.items() if e2 != e])

    def wait_all_dma(self, eng):
        self._pseudo(eng, list(self.dma_ops))

    def emit(self, esems):
        nc = self.nc
        for e in ENGS:
            c = 0
            for o in self.ops[e]:
                if o.dma:
                    continue
                if o.signal:
                    assert o.fn is not None
                    c += 1
                    o.sem = esems[e]
                    o.semval = c
        stats = {}
        with nc.Block() as block:
            def run(engname, eng):
                nw = 0
                for o in self.ops[engname]:
                    for d in o.waits:
                        eng.wait_ge(d.sem, d.semval)
                        nw += 1
                    if o.fn is None:
                        continue
                    inst = o.fn(eng)
                    if o.signal:
                        inst.then_inc(o.sem, o.inc)
                stats[engname] = (len(self.ops[engname]), nw)

            @block.tensor
            def _(eng):
                run("pe", eng)

            @block.scalar
            def _(eng):
                run("act", eng)

            @block.vector
            def _(eng):
                run("dve", eng)

            @block.gpsimd
            def _(eng):
                run("pool", eng)

            @block.sync
            def _(eng):
                run("sp", eng)
        return stats


L = 4
EPS = 1e-6
SLOPES = [float(2.0 ** (-8.0 * (h + 1) / 16.0)) for h in range(16)]
NEGBIG = -30000.0
NS = 4
TMAX = 1152
CQ, CK, CV, CCH, CCB, CCC, CGA, CGB = 0, 1024, 1280, 1536, 2560, 3584, 4608, 5632
QPERM_HEADS = []
for _c in range(8):
    _a = [0, 1, 2, 3, 8, 9, 10, 11][_c]
    QPERM_HEADS += [_a, _a + 4]


def build(nlayers=L, npass=2):
    nc = bass.Bass("TRN2", target_bir_lowering=False)

    def din(name, shape):
        return nc.dram_tensor(name, shape, F32, kind="ExternalInput").ap()

    def dout(name, shape):
        return nc.dram_tensor(name, shape, F32, kind="ExternalOutput").ap()

    xp = din("xp", [2048, 1024])
    xs = din("xs", [128, 1024])
    mem = din("mem", [256, 1024])
    cak = din("cak", [4, 4, 128, 256])
    cav = din("cav", [4, 4, 128, 256])
    sconv = din("sconv", [4, 8, 1024])
    cmk = din("cmk", [4, 4, 256, 256])
    cmv = din("cmv", [4, 4, 256, 256])
    gall = din("gall", [29, 1024])
    sink = din("sink", [4, 16])
    w_in = din("w_in", [4, 1024, 6656])
    wao = din("wao", [4, 1024, 1024])
    wco = din("wco", [4, 1024, 1024])
    wmix = din("wmix", [4, 1024, 1024])
    wcq = din("wcq", [4, 1024, 256])
    wckv = din("wckv", [4, 1024, 512])
    wcox = din("wcox", [4, 256, 1024])
    wup = din("wup", [4, 1024, 4096])
    wdown = din("wdown", [4, 4096, 1024])
    yp = dout("yp", [2048, 1024])
    ys = dout("ys", [128, 1024])
    kp = dout("kp", [4, 128, 256])
    vp = dout("vp", [4, 128, 256])
    convp = dout("convp", [4, 2, 1024])
    mkp = dout("mkp", [4, 256, 256])
    mvp = dout("mvp", [4, 256, 256])
    ksn = dout("ksn", [4, 128, 256])
    vsn = dout("vsn", [4, 128, 256])
    convs = dout("convs", [4, 8, 1024])

    P = Prog(nc)
    with contextlib.ExitStack() as st:
        def sb(name, shape, dt):
            return st.enter_context(nc.sbuf_tensor(name, shape, dt))

        esems = {e: st.enter_context(nc.semaphore("s_" + e)) for e in ENGS}
        wsems = [DSem(st.enter_context(nc.semaphore("w%d" % i))) for i in range(NS)]
        msems = [DSem(st.enter_context(nc.semaphore("m%d" % i))) for i in range(24)]
        mcount = [0]

        psems = [DSem(st.enter_context(nc.semaphore("q%d" % i))) for i in range(16)]
        pcount = [0]

        def ds(eng="sp"):
            if eng == "pool":
                pcount[0] += 1
                return psems[pcount[0] % len(psems)]
            mcount[0] += 1
            return msems[mcount[0] % len(msems)]

        ident = sb("ident", [128, 128], F32)
        identb = sb("identb", [128, 128], BF16)
        onesb = sb("onesb", [128, 128], BF16)
        gT = sb("gT", [128, 8, 29], F32)
        esink = sb("esink", [128, 64], F32)
        ndP = sb("ndP", [128, 2, 128], F32)
        ndS = sb("ndS", [128, 2, 128], F32)
        xT = sb("xT", [128, 8, TMAX], F32)
        hb = sb("hb", [128, 8, TMAX], BF16)
        NA = 32384
        arena = sb("arena", [128, NA], BF16)
        wslot = [sb("wslot%d" % i, [128, 4096], BF16) for i in range(NS)]
        xtok = sb("xtok", [128, 1024], F32)
        gtok = xtok[0:29, :]
        sctok = xtok[0:8, :]
        tA = sb("tA", [128, 512], F32)
        tB = sb("tB", [128, 512], F32)
        tC = sb("tC", [128, 512], F32)
        tD = sb("tD", [128, 512], F32)
        sq = sb("sq", [128, 2, 512], BF16)
        ubuf = sb("ubuf", [128, 1026], F32)
        usb = sb("usb", [128, 4, 34], F32)
        PT = sb("PT", [128, 3, 1024], BF16)
        atok = sb("atok", [128, 2, 1024], BF16)
        den = sb("den", [128, 2, 4], F32)
        epsT = sb("epsT", [128, 1], F32)
        kst = sb("kst", [128, 4, 2, 128], BF16)
        vst = sb("vst", [128, 4, 4, 65], BF16)
        ust = sb("ust", [128, 4, 8, 2], F32)
        mkTp = sb("mkTp", [128, 4, 2, 256], BF16)
        mvp_sb = sb("mvp_sb", [128, 4, 2, 4, 65], BF16)
        tinA = sb("tinA", [128, 4, 256], BF16)
        kcT = sb("kcT", [128, 4, 2, 128], BF16)
        vc = sb("vc", [128, 4, 4, 65], BF16)
        vnew = sb("vnew", [128, 4, 4, 65], BF16)
        scT = sb("scT", [128, 8, 8], F32)
        cvoS = sb("cvoS", [128, 8, 8], F32)
        cvoP = sb("cvoP", [128, 8, 2], F32)
        pf = [st.enter_context(nc.psum_tensor("pf%d" % i, [128, 512], F32)) for i in range(6)]
        pbt = [st.enter_context(nc.psum_tensor("pbt%d" % i, [128, 1024], BF16)) for i in range(2)]

        def av(off, n):
            return arena[:, off:off + n]

        m_ = av(0, 9216).rearrange("p (c t) -> p c t", c=8)
        qa = av(9216, 9216).rearrange("p (c t) -> p c t", c=8)
        kT = av(18432, 2304).rearrange("p (c t) -> p c t", c=2)
        vsb = av(20736, 2432)[:, 0:9 * 260].rearrange("p (t g d) -> p t g d", t=9, g=4)
        cbc = av(23168, 9216).rearrange("p (c t) -> p c t", c=8)
        atoks = arena[0:32, 0:4096]
        cqT = av(9216, 2304).rearrange("p (c t) -> p c t", c=2)
        PTx = av(11520, 4096).rearrange("p (s n) -> p s n", s=8)
        coT = av(15616, 2304).rearrange("p (c t) -> p c t", c=2)
        mkTs = av(17920, 2048).rearrange("p (b c k) -> p b c k", b=4, c=2)
        mvs = av(19968, 2080).rearrange("p (b k h d) -> p b k h d", b=4, k=2, h=4)
        tinM = av(22048, 2048).rearrange("p (b k f) -> p b k f", b=4, k=2)
        PTx2 = av(24096, 4096).rearrange("p (s n) -> p s n", s=8)
        hid = av(0, 16 * TMAX).rearrange("p (c t) -> p c t", c=16)

        def OP(eng, meth, reads, writes, *a, **k):
            P.op(eng, lambda e: getattr(e, meth)(*a, **k), reads=reads, writes=writes)

        def DMA(eng, out, in_, reads, writes, **k):
            P.op(eng, lambda e: e.dma_start(out=out, in_=in_, **k), reads=reads, writes=writes, dma=True, dsem=ds(eng))

        def MM(bank, out, lhsT, rhs, start, stop, reads):
            P.op("pe", lambda e: e.matmul(out, lhsT=lhsT, rhs=rhs, start=start, stop=stop),
                 reads=reads, writes=[("ps", bank)])

        def TRF(bank, out, in_, idn, reads):
            P.op("pe", lambda e: e.transpose(out, in_, idn), reads=reads, writes=[("ps", bank)])

        def TRB(bank, out, in_, idn, reads):
            P.op("pe", lambda e: e.transpose(out, in_, idn), reads=reads, writes=[("pt", bank)])

        bankc = [0]
        stepc = [0]

        bmode = ["all"]
        bankS = [0]

        def nb():
            if bmode[0] == "split":
                bankc[0] = (bankc[0] + 1) % 3
            elif bmode[0] == "split2":
                bankc[0] = (bankc[0] + 1) % 2
            else:
                bankc[0] = (bankc[0] + 1) % 6
            return bankc[0]

        def nbS():
            if bmode[0] == "split":
                bankS[0] = (bankS[0] + 1) % 2
                return 3 + bankS[0]
            if bmode[0] == "split2":
                bankS[0] = (bankS[0] + 1) % 3
                return 2 + bankS[0]
            return nb()

        def nbO():
            if bmode[0] in ("split", "split2"):
                return 5
            return nb()

        btc = [0]

        def nbt():
            btc[0] = (btc[0] + 1) % 2
            return btc[0]

        def keys(name, cs, tiles):
            return [(name, c, t) for c in cs for t in tiles]

        def gt(n0, n1):
            return range(n0 // 128, (n1 + 127) // 128)

        units = []

        def U(tag, src, KC, NCc):
            units.append((tag, src, KC, NCc))

        def wsrc(w2d, r0, kc, c0, ncols):
            return w2d[r0:r0 + kc * 128, c0:c0 + ncols].rearrange("(k p) n -> p k n", p=128)

        for l in range(nlayers):
            U("ckv", wsrc(wckv[l], 0, 8, 0, 512), 8, 512)
        for pas in range(npass):
            for l in range(nlayers):
                for qh in range(2):
                    U("q", wsrc(w_in[l], 0, 8, CQ + qh * 512, 512), 8, 512)
                U("kv", wsrc(w_in[l], 0, 8, CK, 512), 8, 512)
                for fg in range(2):
                    U("ch", wsrc(w_in[l], 0, 8, CCH + fg * 512, 512), 8, 512)
                    U("cc", wsrc(w_in[l], 0, 8, CCC + fg * 512, 512), 8, 512)
                    U("cb", wsrc(w_in[l], 0, 8, CCB + fg * 512, 512), 8, 512)
                for hf in range(2):
                    U("gb", wsrc(w_in[l], 0, 8, CGB + hf * 512, 512), 8, 512)
                    U("wco", wsrc(wco[l], 0, 8, hf * 512, 512), 8, 512)
                for hf in range(2):
                    U("ga", wsrc(w_in[l], 0, 8, CGA + hf * 512, 512), 8, 512)
                    U("wao", wsrc(wao[l], 0, 8, hf * 512, 512), 8, 512)
                for hf in range(2):
                    U("wmix", wsrc(wmix[l], 0, 8, hf * 512, 512), 8, 512)
                U("wcq", wsrc(wcq[l], 0, 8, 0, 256), 8, 256)
                U("wcox", wsrc(wcox[l], 0, 2, 0, 1024), 2, 1024)
                for hf in range(2):
                    for uu in range(4):
                        U("up", wsrc(wup[l], 0, 8, (hf * 4 + uu) * 512, 512), 8, 512)
                    for j in range(8):
                        U("down", wsrc(wdown[l], hf * 2048, 16, j * 128, 128), 16, 128)
        wstate = {"issued": 0, "used": 0}

        def issue_to(n):
            while wstate["issued"] < min(n, len(units)):
                i = wstate["issued"]
                tag, src, KC, NCc = units[i]
                s = i % NS
                dst = wslot[s][:, 0:KC * NCc].rearrange("p (k n) -> p k n", k=KC)
                P.op("pool", lambda e, dst=dst, src=src: e.dma_start(out=dst, in_=src),
                     writes=[("w", s)], dma=True, dsem=wsems[s])
                wstate["issued"] += 1

        def release(n):
            issue_to(wstate["used"] + n)

        def next_unit(tag, keep=1):
            i = wstate["used"]
            assert units[i][0] == tag, (units[i][0], tag)
            wstate["used"] += 1
            issue_to(i - (keep - 1) + NS)
            s = i % NS
            KC, NCc = units[i][2], units[i][3]
            return wslot[s][:, 0:KC * NCc].rearrange("p (k n) -> p k n", k=KC), ("w", s)

        OP("pool", "memset", [], ["ident"], ident[:], 1.0)
        OP("pool", "affine_select", ["ident"], ["ident"], out=ident[:], in_=ident[:], pattern=[[-1, 128]],
           compare_op=ALU.is_equal, fill=0.0, base=0, channel_multiplier=1)
        OP("dve", "tensor_copy", ["ident"], ["identb"], identb[:], ident[:])
        OP("pool", "memset", [], ["onesb"], onesb[:], 1.0)
        OP("pool", "memset", [], ["epsT"], epsT[:], EPS)
        OP("pool", "memset", [], ["u"], ubuf[:], 0.0)
        OP("pool", "memset", [], ["us"], usb[:], 0.0)
        OP("pool", "memset", [], ["vst"], vst[:], 1.0)
        OP("pool", "memset", [], ["mvp_sb"], mvp_sb[:], 1.0)
        OP("pool", "memset", [], ["vc"], vc[:], 1.0)
        OP("pool", "memset", [], ["vnew"], vnew[:], 1.0)
        DMA("sp", gtok, gall, [], ["xtok0", "xtok1"])
        DMA("sp", esink[:], sink.rearrange("l h -> (l h)").partition_broadcast(128), [], ["esink"])
        OP("act", "activation", ["esink"], ["esink"], out=esink[:], in_=esink[:], func=AF.Exp)
        b0 = nb()
        for c in range(8):
            TRF(b0, pf[b0][:, c * 29:(c + 1) * 29], gtok[:, c * 128:(c + 1) * 128], ident[0:29, 0:29], ["xtok0", "xtok1", "ident"])
        OP("act", "activation", [], [("ps", b0), "gT"], out=gT[:].rearrange("p a b -> p (a b)"), in_=pf[b0][:, 0:232], func=AF.Copy)
        OP("pool", "iota", [], ["ndP"], ndP[:, 0, :], pattern=[[1, 128]], base=128, channel_multiplier=-1, allow_small_or_imprecise_dtypes=True)
        OP("pool", "iota", ["ndP"], ["ndP"], ndP[:, 1, :], pattern=[[1, 128]], base=0, channel_multiplier=-1, allow_small_or_imprecise_dtypes=True)
        OP("act", "activation", ["ndP"], ["ndP"], out=ndP[:], in_=ndP[:], func=AF.Abs)
        OP("dve", "tensor_scalar", ["ndP"], ["ndP"], out=ndP[:], in0=ndP[:], scalar1=-1.0, scalar2=None, op0=ALU.mult)
        OP("pool", "memset", ["ndP"], ["ndP"], ndP[0:64, 0, 64:128], NEGBIG)
        OP("pool", "memset", ["ndP"], ["ndP"], ndP[64:128, 1, 0:64], NEGBIG)
        OP("pool", "iota", [], ["ndS"], ndS[:, 0, :].rearrange("p (b i) -> p b i", b=4), pattern=[[0, 4], [1, 32]], base=128, channel_multiplier=-1, allow_small_or_imprecise_dtypes=True)
        OP("pool", "iota", ["ndS"], ["ndS"], ndS[:, 1, :].rearrange("p (b i) -> p b i", b=4), pattern=[[0, 4], [1, 32]], base=0, channel_multiplier=-1, allow_small_or_imprecise_dtypes=True)
        OP("act", "activation", ["ndS"], ["ndS"], out=ndS[:], in_=ndS[:], func=AF.Abs)
        OP("dve", "tensor_scalar", ["ndS"], ["ndS"], out=ndS[:], in0=ndS[:], scalar1=-1.0, scalar2=None, op0=ALU.mult)

        def rms(gi, groups, src=xT, dst=hb, xname="x", hname="h", toff=0, hoff=0):
            for (n0, n1) in groups:
                N = n1 - n0
                tl = list(gt(n0, n1))
                b = nb()
                for c in range(8):
                    s2 = c % 2
                    OP("act", "activation", keys(xname, [c], tl), [("sq", s2)], out=sq[:, s2, 0:N], in_=src[:, c, n0:n1], func=AF.Square)
                    MM(b, pf[b][:, 0:N], onesb[:], sq[:, s2, 0:N], c == 0, c == 7, [("sq", s2), "onesb"])
                OP("act", "activation", ["epsT"], [("ps", b), "tA"], out=tA[:, 0:N], in_=pf[b][:, 0:N], func=AF.Ln, scale=1.0 / 1024.0, bias=epsT[:])
                OP("act", "activation", ["tA"], ["tA"], out=tA[:, 0:N], in_=tA[:, 0:N], func=AF.Exp, scale=-0.5)
                for c in range(8):
                    OP("dve", "scalar_tensor_tensor", keys(xname, [c], tl) + ["tA", "gT"], keys(hname, [c], [t + hoff for t in tl]),
                       out=dst[:, c, n0 + hoff * 128:n1 + hoff * 128], in0=src[:, c, n0:n1], scalar=gT[:, c, gi:gi + 1], in1=tA[:, 0:N], op0=ALU.mult, op1=ALU.mult)

        import os as _os
        _STOP = int(_os.environ.get('KSTOP', '0'))

        class _Stop(Exception):
            pass

        def stage(n):
            if n == _STOP:
                raise _Stop()

        def _body():
            def xload(ti, kind, gi_):
                src = xp[gi_ * 128:(gi_ + 1) * 128, :] if kind == "p" else xs
                for half in range(2):
                    hk = "xtok%d" % half
                    DMA("sp", xtok[:, half * 512:(half + 1) * 512], src[:, half * 512:(half + 1) * 512], [], [hk])
                    b = nb()
                    for i in range(4):
                        c = half * 4 + i
                        TRF(b, pf[b][:, i * 128:(i + 1) * 128], xtok[:, c * 128:(c + 1) * 128], ident[:], [hk, "ident"])
                    OP("act", "activation", [], [("ps", b)] + keys("x", range(half * 4, half * 4 + 4), [ti]),
                       out=xT[:, half * 4:half * 4 + 4, ti * 128:(ti + 1) * 128], in_=pf[b][:].rearrange("p (c t) -> p c t", c=4), func=AF.Copy)

            NPRE = 5
            for ti in range(NPRE):
                xload(ti, "p", ti)
            for kt in range(2):
                DMA("sp", xtok[:], mem[kt * 128:(kt + 1) * 128, :], [], ["xtok0", "xtok1"])
                for half in range(2):
                    b = nb()
                    for i in range(4):
                        c = half * 4 + i
                        TRF(b, pf[b][:, i * 128:(i + 1) * 128], xtok[:, c * 128:(c + 1) * 128], ident[:], ["xtok0", "xtok1", "ident"])
                    OP("act", "activation", [], [("ps", b)] + keys("x", range(half * 4, half * 4 + 4), [5 + kt]),
                       out=xT[:, half * 4:half * 4 + 4, (5 + kt) * 128:(6 + kt) * 128], in_=pf[b][:].rearrange("p (c t) -> p c t", c=4), func=AF.Copy)
            if True:
                N = 256
                b = nb()
                for c in range(8):
                    s2 = c % 2
                    OP("act", "activation", keys("x", [c], [5, 6]), [("sq", s2)], out=sq[:, s2, 0:N], in_=xT[:, c, 640:896], func=AF.Square)
                    MM(b, pf[b][:, 0:N], onesb[:], sq[:, s2, 0:N], c == 0, c == 7, [("sq", s2), "onesb"])
                OP("act", "activation", ["epsT"], [("ps", b), "tA"], out=tA[:, 0:N], in_=pf[b][:, 0:N], func=AF.Ln, scale=1.0 / 1024.0, bias=epsT[:])
                OP("act", "activation", ["tA"], ["tA"], out=tA[:, 0:N], in_=tA[:, 0:N], func=AF.Exp, scale=-0.5)
                for c in range(8):
                    OP("dve", "tensor_tensor", keys("x", [c], [5, 6]) + ["tA"], keys("x", [c], [7, 8]),
                       out=xT[:, c, 896:1152], in0=xT[:, c, 640:896], in1=tA[:, 0:N], op=ALU.mult)
            for l in range(nlayers):
                ho = 640 + (l % 2) * 256
                ht = [5 + (l % 2) * 2, 6 + (l % 2) * 2]
                for c in range(8):
                    OP("dve", "tensor_scalar", keys("x", [c], [7, 8]) + ["gT"], keys("h", [c], ht),
                       out=hb[:, c, ho:ho + 256], in0=xT[:, c, 896:1152], scalar1=gT[:, c, 12 + l:13 + l], scalar2=None, op0=ALU.mult)
                W, wk = next_unit("ckv")
                for kt in range(2):
                    b = nb()
                    for kc in range(8):
                        MM(b, pf[b][:, 0:512], hb[:, kc, ho + kt * 128:ho + (kt + 1) * 128], W[:, kc, :], kc == 0, kc == 7, keys("h", [kc], ht) + [wk])
                    OP("act", "activation", [], [("ps", b), "tD"], out=tD[:], in_=pf[b][:], func=AF.Copy)
                    OP("dve", "tensor_copy", ["tD"], ["mvp_sb"], mvp_sb[:, l, kt, :, 0:64], tD[:, 256:512].rearrange("p (h d) -> p h d", h=4))
                    DMA("sp", mkp[l, kt * 128:(kt + 1) * 128, :], tD[:, 0:256], ["tD"], [])
                    DMA("sp", mvp[l, kt * 128:(kt + 1) * 128, :], tD[:, 256:512], ["tD"], [])
                for c in range(2):
                    b = nb()
                    for kc in range(8):
                        MM(b, pf[b][:, 0:256], W[:, kc, c * 128:(c + 1) * 128], hb[:, kc, ho:ho + 256], kc == 0, kc == 7, keys("h", [kc], ht) + [wk])
                    OP("act", "activation", [], [("ps", b), "mkTp"], out=mkTp[:, l, c, :], in_=pf[b][:, 0:256], func=AF.Copy)
            stage(1)

            for pas in range(npass):
                if pas == 0:
                    tiles = [("p", i) for i in range(8)] + [("s", 0)]
                else:
                    tiles = [("p", i) for i in range(8, 16)]
                NT = len(tiles)
                T = NT * 128
                groups = [(n0, min(n0 + 512, T)) for n0 in range(0, T, 512)]
                has_s = (pas == 0)
                s_ti = 8

                for ti, (kind, gi_) in enumerate(tiles):
                    if pas == 0 and ti < NPRE:
                        continue
                    xload(ti, kind, gi_)

                stage(2)
                for l in range(nlayers):
                    if has_s:
                        P.op("pool", lambda e, l=l: e.dma_start(out=tinA[:], in_=cak[l].rearrange("b p f -> p b f")), writes=["tinA"], dma=True, dsem=ds("pool"))
                        for b4 in range(4):
                            P.op("pool", lambda e, l=l, b4=b4: e.dma_start(out=vc[:, b4, :, 0:64], in_=cav[l, b4].rearrange("p (g d) -> p g d", g=4)),
                                 writes=[("vc", b4)], dma=True, dsem=ds("pool"))
                        DMA("sp", sctok, sconv[l], [], ["xtok0", "xtok1"])

                    stage(3)
                    if l == 0:
                        rms(l, groups)
                    stage(4)

                    for qh in range(2):
                        Wq, kq = next_unit("q")
                        for jj in range(4):
                            cq_ = qh * 4 + jj
                            for (n0, n1) in groups:
                                N = n1 - n0
                                tl = list(gt(n0, n1))
                                b = nb()
                                for kc in range(8):
                                    MM(b, pf[b][:, 0:N], Wq[:, kc, jj * 128:(jj + 1) * 128], hb[:, kc, n0:n1], kc == 0, kc == 7, keys("h", [kc], tl) + [kq])
                                OP("act", "activation", [], [("ps", b)] + keys("qa", [cq_], tl), out=qa[:, cq_, n0:n1], in_=pf[b][:, 0:N], func=AF.Copy, scale=0.125)
                    Wkv, kkv = next_unit("kv")
                    for c in range(2):
                        for (n0, n1) in groups:
                            N = n1 - n0
                            tl = list(gt(n0, n1))
                            b = nb()
                            for kc in range(8):
                                MM(b, pf[b][:, 0:N], Wkv[:, kc, c * 128:(c + 1) * 128], hb[:, kc, n0:n1], kc == 0, kc == 7, keys("h", [kc], tl) + [kkv])
                            OP("dve", "tensor_copy", [], [("ps", b)] + keys("kT", [c], tl), kT[:, c, n0:n1], pf[b][:, 0:N])
                    for ti, (kind, gi_) in enumerate(tiles):
                        need_out = (kind == "s") or (gi_ == 15)
                        b = nb()
                        if need_out:
                            for kc in range(8):
                                MM(b, pf[b][:, 0:512], hb[:, kc, ti * 128:(ti + 1) * 128], Wkv[:, kc, 0:512], kc == 0, kc == 7, keys("h", [kc], [ti]) + [kkv])
                        else:
                            for kc in range(8):
                                MM(b, pf[b][:, 256:512], hb[:, kc, ti * 128:(ti + 1) * 128], Wkv[:, kc, 256:512], kc == 0, kc == 7, keys("h", [kc], [ti]) + [kkv])
                        OP("dve", "tensor_copy", [], [("ps", b), ("v", ti)], vsb[:, ti, :, 0:64], pf[b][:, 256:512].rearrange("p (g d) -> p g d", g=4))
                        if need_out:
                            OP("act", "activation", [], [("ps", b), "tD"], out=tD[:], in_=pf[b][:], func=AF.Copy)
                            if kind == "s":
                                DMA("sp", ksn[l], tD[:, 0:256], ["tD"], [])
                                DMA("sp", vsn[l], tD[:, 256:512], ["tD"], [])
                            else:
                                DMA("sp", kp[l], tD[:, 0:256], ["tD"], [])
                                DMA("sp", vp[l], tD[:, 256:512], ["tD"], [])
                        if kind == "s":
                            for b4 in range(4):
                                bb = nb()
                                for kc in range(8):
                                    MM(bb, pf[bb][0:32, 0:256], hb[:, kc, ti * 128 + b4 * 32:ti * 128 + (b4 + 1) * 32], Wkv[:, kc, 256:512], kc == 0, kc == 7, keys("h", [kc], [ti]) + [kkv])
                                OP("dve", "tensor_copy", [], [("ps", bb), ("vnew", b4)], vnew[0:32, b4, :, 0:64], pf[bb][0:32, 0:256].rearrange("p (g d) -> p g d", g=4))
                    OP("dve", "memset", [], [("v1", 0)], vsb[:, :, :, 64:65], 1.0)
                    if pas == 0 and npass == 2:
                        OP("dve", "tensor_copy", keys("kT", [0, 1], [7]), ["kst"], kst[:, l, :, :], kT[:, :, 7 * 128:8 * 128])
                        OP("dve", "tensor_copy", [("v", 7)], ["vst"], vst[:, l, :, 0:64], vsb[:, 7, :, 0:64])
                    stage(7)

                    if has_s:
                        b = nb()
                        for j in range(8):
                            TRF(b, pf[b][:, j * 8:(j + 1) * 8], sctok[:, j * 128:(j + 1) * 128], ident[0:8, 0:8], ["xtok0", "xtok1", "ident"])
                        OP("act", "activation", [], [("ps", b), "scT"], out=scT[:].rearrange("p a b -> p (a b)"), in_=pf[b][:, 0:64], func=AF.Copy)
                        for b4 in range(4):
                            bt = nbt()
                            for c in range(2):
                                TRB(bt, pbt[bt][:, c * 128:(c + 1) * 128], tinA[:, b4, c * 128:(c + 1) * 128], identb[:], ["tinA", "identb"])
                            OP("act", "activation", [], [("pt", bt), ("kcT", b4)], out=kcT[:, b4, :, :].rearrange("p c k -> p (c k)"), in_=pbt[bt][:, 0:256], func=AF.Copy)

                    steps = []

                    def mk_prompt_tile(ti):
                        cols = slice(ti * 128, (ti + 1) * 128)
                        par = ti % 2
                        hasA = (ti > 0) or (pas == 1)
                        for g in range(4):
                            st_ = {}

                            def S(g=g, st_=st_):
                                kc_ = g // 2
                                ph = (g % 2) * 64
                                sp_ = stepc[0] % 3
                                stepc[0] += 1
                                st_["sp"] = sp_
                                if hasA:
                                    if ti > 0:
                                        kA = kT[ph:ph + 64, kc_, (ti - 1) * 128:ti * 128]
                                        rA = [("kT", kc_, ti - 1)]
                                    else:
                                        kA = kst[ph:ph + 64, l, kc_, :]
                                        rA = ["kst"]
                                for hp in range(2):
                                    Sb = nbS()
                                    pk = ("PT", sp_, hp)
                                    PTh = PT[:, sp_, hp * 512:(hp + 1) * 512]
                                    for hh in range(2):
                                        h = 4 * g + 2 * hp + hh
                                        cq_ = (h // 8) * 4 + (h % 4)
                                        qap = qa[ph:ph + 64, cq_, cols]
                                        if hasA:
                                            MM(Sb, pf[Sb][:, hh * 256:hh * 256 + 128], kA, qap, True, True, rA + [("qa", cq_, ti)])
                                        MM(Sb, pf[Sb][:, hh * 256 + 128:hh * 256 + 256], kT[ph:ph + 64, kc_, cols], qap, True, True, [("kT", kc_, ti), ("qa", cq_, ti)])
                                    for hh in range(2):
                                        h = 4 * g + 2 * hp + hh
                                        lo = hh * 256 + (0 if hasA else 128)
                                        hi = hh * 256 + 256
                                        ndv = ndP[:].rearrange("p a b -> p (a b)")[:, (0 if hasA else 128):256]
                                        OP("dve", "scalar_tensor_tensor", ["ndP"], [("ps", Sb)], out=pf[Sb][:, lo:hi], in0=ndv, scalar=SLOPES[h], in1=pf[Sb][:, lo:hi], op0=ALU.mult, op1=ALU.add)
                                        if not hasA:
                                            OP("act", "activation", [], [("ps", Sb), pk + (hh, 0), pk + (hh, 1)], out=PTh[:, lo:hi], in_=pf[Sb][:, lo:hi], func=AF.Exp)
                                    if hasA:
                                        OP("act", "activation", [], [("ps", Sb), pk + (0, 0), pk + (0, 1), pk + (1, 0), pk + (1, 1)], out=PTh[:, 0:512], in_=pf[Sb][:, 0:512], func=AF.Exp)

                            def V(g=g, st_=st_):
                                sp_ = st_["sp"]
                                if hasA:
                                    if ti > 0:
                                        vA = vsb[:, ti - 1, g, :]
                                        rvA = [("v", ti - 1), ("v1", 0)]
                                    else:
                                        vA = vst[:, l, g, :]
                                        rvA = ["vst"]
                                Ob = nbO()
                                for hp in range(2):
                                    pk = ("PT", sp_, hp)
                                    PTh = PT[:, sp_, hp * 512:(hp + 1) * 512]
                                    for hh in range(2):
                                        hs = 2 * hp + hh
                                        oo = pf[Ob][:, hs * 65:(hs + 1) * 65]
                                        if hasA:
                                            MM(Ob, oo, PTh[:, hh * 256:hh * 256 + 128], vA, True, False, [pk + (hh, 0)] + rvA)
                                        MM(Ob, oo, PTh[:, hh * 256 + 128:hh * 256 + 256], vsb[:, ti, g, :], (not hasA), True, [pk + (hh, 1), ("v", ti), ("v1", 0)])
                                O = pf[Ob][:, 0:260].rearrange("p (h d) -> p h d", h=4)
                                dn = den[:, g % 2, :]
                                dk = ("den", g % 2)
                                OP("dve", "tensor_tensor", ["esink"], [("ps", Ob), dk], out=dn, in0=O[:, :, 64], in1=esink[:, l * 16 + 4 * g:l * 16 + 4 * g + 4], op=ALU.add)
                                OP("dve", "reciprocal", [dk], [dk], dn, dn)
                                OP("dve", "tensor_tensor", [dk], [("ps", Ob), ("atok", par)], out=atok[:, par, g * 256:(g + 1) * 256].rearrange("p (h d) -> p h d", h=4),
                                   in0=O[:, :, 0:64], in1=dn.unsqueeze(2).to_broadcast([128, 4, 64]), op=ALU.mult)

                            steps.append((S, V))

                        def TAIL():
                            bt = nbt()
                            for c in range(8):
                                TRB(bt, pbt[bt][:, c * 128:(c + 1) * 128], atok[:, par, c * 128:(c + 1) * 128], identb[:], [("atok", par), "identb"])
                            OP("act", "activation", [], [("pt", bt)] + keys("qa", range(8), [ti]), out=qa[:, :, cols], in_=pbt[bt][:].rearrange("p (c t) -> p c t", c=8), func=AF.Copy)

                        steps.append((None, TAIL))

                    def mk_sample_tile(ti):
                        cols = slice(ti * 128, (ti + 1) * 128)
                        for g in range(4):
                            st_ = {}

                            def S(g=g, st_=st_):
                                kc_ = g // 2
                                ph = (g % 2) * 64
                                sp_ = stepc[0] % 3
                                stepc[0] += 1
                                st_["sp"] = sp_
                                for hp in range(2):
                                    Sb = nbS()
                                    pk = ("PT", sp_, hp)
                                    PTh = PT[:, sp_, hp * 512:(hp + 1) * 512]
                                    for hh in range(2):
                                        h = 4 * g + 2 * hp + hh
                                        cq_ = (h // 8) * 4 + (h % 4)
                                        for b4 in range(4):
                                            qap = qa[ph:ph + 64, cq_, ti * 128 + b4 * 32:ti * 128 + (b4 + 1) * 32]
                                            MM(Sb, pf[Sb][:, hh * 256 + b4 * 32:hh * 256 + (b4 + 1) * 32], kcT[ph:ph + 64, b4, kc_, :], qap, True, True, [("kcT", b4), ("qa", cq_, ti)])
                                            MM(Sb, pf[Sb][0:32, hh * 256 + 128 + b4 * 32:hh * 256 + 128 + (b4 + 1) * 32],
                                               kT[ph:ph + 64, kc_, ti * 128 + b4 * 32:ti * 128 + (b4 + 1) * 32], qap, True, True, [("kT", kc_, ti), ("qa", cq_, ti)])
                                    for hh in range(2):
                                        h = 4 * g + 2 * hp + hh
                                        lo = hh * 256
                                        OP("dve", "scalar_tensor_tensor", ["ndS"], [("ps", Sb)], out=pf[Sb][:, lo:lo + 128], in0=ndS[:, 0, :], scalar=SLOPES[h], in1=pf[Sb][:, lo:lo + 128], op0=ALU.mult, op1=ALU.add)
                                        OP("dve", "scalar_tensor_tensor", ["ndS"], [("ps", Sb)], out=pf[Sb][0:32, lo + 128:lo + 256], in0=ndS[0:32, 1, :], scalar=SLOPES[h], in1=pf[Sb][0:32, lo + 128:lo + 256], op0=ALU.mult, op1=ALU.add)
                                        OP("act", "activation", [], [("ps", Sb), pk + (hh, 0)], out=PTh[:, lo:lo + 128], in_=pf[Sb][:, lo:lo + 128], func=AF.Exp)
                                        OP("act", "activation", [], [("ps", Sb), pk + (hh, 1)], out=PTh[0:32, lo + 128:lo + 256], in_=pf[Sb][0:32, lo + 128:lo + 256], func=AF.Exp)

                            def V(g=g, st_=st_):
                                sp_ = st_["sp"]
                                for b4 in range(4):
                                    Ob = nbO()
                                    for hs in range(4):
                                        hp, hh = hs // 2, hs % 2
                                        PTh = PT[:, sp_, hp * 512:(hp + 1) * 512]
                                        oo = pf[Ob][0:32, hs * 65:(hs + 1) * 65]
                                        MM(Ob, oo, PTh[:, hh * 256 + b4 * 32:hh * 256 + (b4 + 1) * 32], vc[:, b4, g, :], True, False, [("PT", sp_, hp, hh, 0), ("vc", b4)])
                                        MM(Ob, oo, PTh[0:32, hh * 256 + 128 + b4 * 32:hh * 256 + 128 + (b4 + 1) * 32], vnew[0:32, b4, g, :], False, True, [("PT", sp_, hp, hh, 1), ("vnew", b4)])
                                    O = pf[Ob][0:32, 0:260].rearrange("p (h d) -> p h d", h=4)
                                    dn = den[0:32, b4 % 2, :]
                                    dk = ("den", b4 % 2)
                                    OP("dve", "tensor_tensor", ["esink"], [("ps", Ob), dk], out=dn, in0=O[:, :, 64], in1=esink[0:32, l * 16 + 4 * g:l * 16 + 4 * g + 4], op=ALU.add)
                                    OP("dve", "reciprocal", [dk], [dk], dn, dn)
                                    OP("dve", "tensor_tensor", [dk], [("ps", Ob), "atoks"], out=atoks[:, b4 * 1024 + g * 256:b4 * 1024 + (g + 1) * 256].rearrange("p (h d) -> p h d", h=4),
                                       in0=O[:, :, 0:64], in1=dn.unsqueeze(2).to_broadcast([32, 4, 64]), op=ALU.mult)

                            steps.append((S, V))

                        def TAIL():
                            bt = nbt()
                            for b4 in range(4):
                                for c in range(8):
                                    TRB(bt, pbt[bt][:, c * 128 + b4 * 32:c * 128 + (b4 + 1) * 32], atoks[:, b4 * 1024 + c * 128:b4 * 1024 + (c + 1) * 128], identb[0:32, 0:32], ["atoks", "identb"])
                            OP("act", "activation", [], [("pt", bt)] + keys("qa", range(8), [ti]), out=qa[:, :, cols], in_=pbt[bt][:].rearrange("p (c t) -> p c t", c=8), func=AF.Copy)

                        steps.append((None, TAIL))

                    for ti, (kind, gi_) in enumerate(tiles):
                        if kind == "s":
                            mk_sample_tile(ti)
                    for ti, (kind, gi_) in enumerate(tiles):
                        if kind == "p":
                            mk_prompt_tile(ti)
                    pipe = {"i": 0, "fly": [None, None]}

                    def tick():
                        nxt = None
                        if pipe["i"] < len(steps):
                            nxt = steps[pipe["i"]]
                            pipe["i"] += 1
                        if nxt is not None and nxt[0] is not None:
                            nxt[0]()
                        old_ = pipe["fly"].pop(0)
                        if old_ is not None:
                            old_[1]()
                        pipe["fly"].append(nxt)

                    def flush():
                        while pipe["i"] < len(steps) or any(f is not None for f in pipe["fly"]):
                            tick()

                    n_sample_steps = 5 if has_s else 0

                    bmode[0] = "all"
                    for fg in range(2):
                        Wch, kch = next_unit("ch")
                        Wcc, kcc = next_unit("cc", 2)
                        Wcb, kcb = next_unit("cb", 3)
                        for jj in range(4):
                            j = fg * 4 + jj
                            if pas == 1:
                                OP("dve", "tensor_copy", ["ust"], ["u"], ubuf[:, 0:2], ust[:, l, j, :])
                            if has_s:
                                OP("dve", "tensor_copy", ["scT"], ["us"], usb[:, :, 0:2], scT[:, j, :].rearrange("p (b r) -> p b r", b=4))
                            for gidx, (n0, n1) in enumerate(groups):
                                N = n1 - n0
                                tl = list(gt(n0, n1))
                                is_s = has_s and gidx == 2
                                b1 = nb()
                                for kc in range(8):
                                    MM(b1, pf[b1][:, 0:N], Wch[:, kc, jj * 128:(jj + 1) * 128], hb[:, kc, n0:n1], kc == 0, kc == 7, keys("h", [kc], tl) + [kch])
                                OP("act", "activation", [], [("ps", b1), "tB"], out=tB[:, 0:N], in_=pf[b1][:, 0:N], func=AF.Copy)
                                b2 = nb()
                                for kc in range(8):
                                    MM(b2, pf[b2][:, 0:N], Wcc[:, kc, jj * 128:(jj + 1) * 128], hb[:, kc, n0:n1], kc == 0, kc == 7, keys("h", [kc], tl) + [kcc])
                                if not is_s:
                                    OP("dve", "tensor_tensor", ["tB"], [("ps", b2), "u"], out=ubuf[:, 2 + n0:2 + n1], in0=pf[b2][:, 0:N], in1=tB[:, 0:N], op=ALU.mult)
                                    u0, u1, u2 = ubuf[:, n0:n1], ubuf[:, n0 + 1:n1 + 1], ubuf[:, n0 + 2:n1 + 2]
                                    cv = tC[:, 0:N]
                                    ukey = "u"
                                else:
                                    OP("dve", "tensor_tensor", ["tB"], [("ps", b2), "us"], out=usb[:, :, 2:34], in0=pf[b2][:, 0:N].rearrange("p (b i) -> p b i", b=4),
                                       in1=tB[:, 0:N].rearrange("p (b i) -> p b i", b=4), op=ALU.mult)
                                    u0, u1, u2 = usb[:, :, 0:32], usb[:, :, 1:33], usb[:, :, 2:34]
                                    cv = tC[:, 0:N].rearrange("p (b i) -> p b i", b=4)
                                    ukey = "us"
                                w0 = gT[:, j, 17 + l * 3 + 0:17 + l * 3 + 1]
                                w1 = gT[:, j, 17 + l * 3 + 1:17 + l * 3 + 2]
                                w2 = gT[:, j, 17 + l * 3 + 2:17 + l * 3 + 3]
                                OP("dve", "tensor_scalar", [ukey, "gT"], ["tC"], out=cv, in0=u0, scalar1=w0, scalar2=None, op0=ALU.mult)
                                OP("dve", "scalar_tensor_tensor", [ukey, "tC", "gT"], ["tC"], out=cv, in0=u1, scalar=w1, in1=cv, op0=ALU.mult, op1=ALU.add)
                                OP("dve", "scalar_tensor_tensor", [ukey, "tC", "gT"], ["tC"], out=cv, in0=u2, scalar=w2, in1=cv, op0=ALU.mult, op1=ALU.add)
                                b3 = nb()
                                for kc in range(8):
                                    MM(b3, pf[b3][:, 0:N], Wcb[:, kc, jj * 128:(jj + 1) * 128], hb[:, kc, n0:n1], kc == 0, kc == 7, keys("h", [kc], tl) + [kcb])
                                OP("dve", "tensor_tensor", ["tC"], [("ps", b3)] + keys("cb", [j], tl), out=cbc[:, j, n0:n1], in0=pf[b3][:, 0:N], in1=tC[:, 0:N], op=ALU.mult)
                                tick()
                            if pas == 0:
                                OP("dve", "tensor_copy", ["u"], ["ust"], ust[:, l, j, :], ubuf[:, 1024:1026])
                                OP("dve", "tensor_copy", ["us"], ["cvoS"], cvoS[:, j, :].rearrange("p (b r) -> p b r", b=4), usb[:, :, 32:34])
                            else:
                                OP("dve", "tensor_copy", ["u"], ["cvoP"], cvoP[:, j, :], ubuf[:, 1024:1026])
                    stage(5)
                    if pas == 0:
                        nr, cvo, dst = 8, cvoS, convs[l]
                    else:
                        nr, cvo, dst = 2, cvoP, convp[l]
                    if pas == 0 or npass == 2:
                        for half in range(2):
                            b = nb()
                            for i in range(4):
                                j = half * 4 + i
                                TRF(b, pf[b][0:nr, i * 128:(i + 1) * 128], cvo[:, j, :], ident[:], ["cvoS" if pas == 0 else "cvoP", "ident"])
                            OP("act", "activation", [], [("ps", b), "xtok0", "xtok1"], out=xtok[0:nr, half * 512:(half + 1) * 512], in_=pf[b][0:nr, :], func=AF.Copy)
                        DMA("sp", dst, xtok[0:nr, :], ["xtok0", "xtok1"], [])

                    if has_s:
                        assert pipe["i"] > n_sample_steps + 2
                    stage(6)
                    bmode[0] = "split"
                    for hf in range(2):
                        Wg, kg = next_unit("gb")
                        Wo, ko = next_unit("wco", 2)
                        for jj in range(4):
                            j = hf * 4 + jj
                            for (n0, n1) in groups:
                                N = n1 - n0
                                tl = list(gt(n0, n1))
                                bg = nb()
                                for kc in range(8):
                                    MM(bg, pf[bg][:, 0:N], Wg[:, kc, jj * 128:(jj + 1) * 128], hb[:, kc, n0:n1], kc == 0, kc == 7, keys("h", [kc], tl) + [kg])
                                OP("act", "activation", [], [("ps", bg), "tB"], out=tB[:, 0:N], in_=pf[bg][:, 0:N], func=AF.Sigmoid)
                                bo = nb()
                                for kc in range(8):
                                    MM(bo, pf[bo][:, 0:N], Wo[:, kc, jj * 128:(jj + 1) * 128], cbc[:, kc, n0:n1], kc == 0, kc == 7, keys("cb", [kc], tl) + [ko])
                                OP("dve", "tensor_tensor", ["tB"], [("ps", bo)] + keys("m", [j], tl), out=m_[:, j, n0:n1], in0=pf[bo][:, 0:N], in1=tB[:, 0:N], op=ALU.mult)
                                tick()
                    flush()
                    bmode[0] = "all"
                    stage(8)
                    stage(9)
                    for hf in range(2):
                        Wg, kg = next_unit("ga")
                        Wo, ko = next_unit("wao", 2)
                        for jj in range(4):
                            j = hf * 4 + jj
                            for (n0, n1) in groups:
                                N = n1 - n0
                                tl = list(gt(n0, n1))
                                bg = nb()
                                for kc in range(8):
                                    MM(bg, pf[bg][:, 0:N], Wg[:, kc, jj * 128:(jj + 1) * 128], hb[:, kc, n0:n1], kc == 0, kc == 7, keys("h", [kc], tl) + [kg])
                                OP("act", "activation", [], [("ps", bg), "tB"], out=tB[:, 0:N], in_=pf[bg][:, 0:N], func=AF.Sigmoid)
                                bo = nb()
                                for kc in range(8):
                                    MM(bo, pf[bo][:, 0:N], Wo[:, kc, jj * 128:(jj + 1) * 128], qa[:, kc, n0:n1], kc == 0, kc == 7, keys("qa", [kc], tl) + [ko])
                                OP("dve", "tensor_tensor", ["tB"], [("ps", bo), "tC"], out=tC[:, 0:N], in0=pf[bo][:, 0:N], in1=tB[:, 0:N], op=ALU.mult)
                                OP("dve", "tensor_tensor", ["tC"] + keys("m", [j], tl), keys("m", [j], tl), out=m_[:, j, n0:n1], in0=m_[:, j, n0:n1], in1=tC[:, 0:N], op=ALU.add)
                    P.barrier()
                    stage(10)
                    if has_s:
                        for b4 in range(4):
                            P.op("pool", lambda e, l=l, b4=b4: e.dma_start(out=tinM[:, b4, :, :], in_=cmk[l, b4].rearrange("(k p) f -> p k f", p=128)),
                                 writes=[("tinM", b4)], dma=True, dsem=ds("pool"))
                            for kt in range(2):
                                P.op("pool", lambda e, l=l, b4=b4, kt=kt: e.dma_start(out=mvs[:, b4, kt, :, 0:64], in_=cmv[l, b4, kt * 128:(kt + 1) * 128, :].rearrange("p (h d) -> p h d", h=4)),
                                     writes=[("mvs", b4)], dma=True, dsem=ds("pool"))
                        OP("dve", "memset", [], [("mvs1", 0)], mvs[:, :, :, :, 64:65].rearrange("p b k h d -> p (b k h) d"), 1.0)

                    def xPREP(gi):
                        for b4 in range(4):
                            bt = nbt()
                            for c in range(2):
                                for kt in range(2):
                                    TRB(bt, pbt[bt][:, (c * 2 + kt) * 128:(c * 2 + kt + 1) * 128], tinM[:, b4, kt, c * 128:(c + 1) * 128], identb[:], [("tinM", b4), "identb"])
                            OP("act", "activation", [], [("pt", bt), ("mkTs", b4)], out=mkTs[:, b4, :, :].rearrange("p c k -> p (c k)"), in_=pbt[bt][:, 0:512], func=AF.Copy)

                    stage(101)
                    Wm0, km0 = next_unit("wmix")
                    Wm1, km1 = next_unit("wmix", 2)
                    Wq, kq = next_unit("wcq", 3)
                    Wx, kx = next_unit("wcox", 4)
                    PTxb = [PTx, PTx2]

                    def xMIX(gi):
                        n0, n1 = groups[gi]
                        N = n1 - n0
                        tl = list(gt(n0, n1))
                        for j in range(8):
                            Wm, km = (Wm0, km0) if j < 4 else (Wm1, km1)
                            jj = j % 4
                            b = nb()
                            for kc in range(8):
                                MM(b, pf[b][:, 0:N], Wm[:, kc, jj * 128:(jj + 1) * 128], m_[:, kc, n0:n1], kc == 0, kc == 7, keys("m", [kc], tl) + [km])
                            OP("dve", "tensor_tensor", keys("x", [j], tl), [("ps", b)] + keys("x", [j], tl), out=xT[:, j, n0:n1], in0=xT[:, j, n0:n1], in1=pf[b][:, 0:N], op=ALU.add)
                        if gi == len(groups) - 1:
                            release(2)

                    def xRMS(gi):
                        rms(4 + l, [groups[gi]])

                    def xCQ(gi):
                        n0, n1 = groups[gi]
                        N = n1 - n0
                        tl = list(gt(n0, n1))
                        for c in range(2):
                            b = nb()
                            for kc in range(8):
                                MM(b, pf[b][:, 0:N], Wq[:, kc, c * 128:(c + 1) * 128], hb[:, kc, n0:n1], kc == 0, kc == 7, keys("h", [kc], tl) + [kq])
                            OP("act", "activation", [], [("ps", b)] + keys("cq", [c], tl), out=cqT[:, c, n0:n1], in_=pf[b][:, 0:N], func=AF.Copy, scale=0.125)

                    def xS(gi):
                        n0, n1 = groups[gi]
                        N = n1 - n0
                        tl = list(gt(n0, n1))
                        PX = PTxb[gi % 2]
                        is_s = has_s and gi == 2
                        if not is_s:
                            for hm in range(4):
                                c = hm // 2
                                ph = (hm % 2) * 64
                                for kt in range(2):
                                    b = nb()
                                    MM(b, pf[b][:, 0:N], mkTp[ph:ph + 64, l, c, kt * 128:(kt + 1) * 128], cqT[ph:ph + 64, c, n0:n1], True, True, ["mkTp"] + keys("cq", [c], tl))
                                    OP("act", "activation", [], [("ps", b), ("PTx", gi % 2, hm * 2 + kt)], out=PX[:, hm * 2 + kt, 0:N], in_=pf[b][:, 0:N], func=AF.Exp)
                        else:
                            for hm in range(4):
                                c = hm // 2
                                ph = (hm % 2) * 64
                                for kt in range(2):
                                    for b4 in range(4):
                                        b = nb()
                                        MM(b, pf[b][:, 0:128], mkTs[ph:ph + 64, b4, c, kt * 128:(kt + 1) * 128], cqT[ph:ph + 64, c, n0:n1], True, True,
                                           [("mkTs", b4)] + keys("cq", [c], tl))
                                        OP("act", "activation", [], [("ps", b), ("PTx", gi % 2, hm * 2 + kt)], out=PX[:, hm * 2 + kt, b4 * 32:(b4 + 1) * 32],
                                           in_=pf[b][:, b4 * 32:(b4 + 1) * 32], func=AF.Exp)

                    def xPV(gi):
                        n0, n1 = groups[gi]
                        tl = list(gt(n0, n1))
                        PX = PTxb[gi % 2]
                        is_s = has_s and gi == 2
                        if not is_s:
                            for ti in tl:
                                par = ti % 2
                                kk = (ti // 2) % 4
                                off = ti * 128 - n0
                                Ob = nb()
                                for hm in range(4):
                                    for kt in range(2):
                                        MM(Ob, pf[Ob][:, hm * 65:(hm + 1) * 65], PX[:, hm * 2 + kt, off:off + 128], mvp_sb[:, l, kt, hm, :], kt == 0, kt == 1, [("PTx", gi % 2, hm * 2 + kt), "mvp_sb"])
                                O = pf[Ob][:, 0:260].rearrange("p (h d) -> p h d", h=4)
                                dn = den[:, par, :]
                                dk = ("den", par)
                                OP("dve", "reciprocal", [], [("ps", Ob), dk], dn, O[:, :, 64])
                                OP("dve", "tensor_tensor", [dk], [("ps", Ob), ("atokx", par, kk)], out=atok[:, par, kk * 256:(kk + 1) * 256].rearrange("p (h d) -> p h d", h=4),
                                   in0=O[:, :, 0:64], in1=dn.unsqueeze(2).to_broadcast([128, 4, 64]), op=ALU.mult)
                        else:
                            for b4 in range(4):
                                Ob = nb()
                                for hm in range(4):
                                    for kt in range(2):
                                        MM(Ob, pf[Ob][0:32, hm * 65:(hm + 1) * 65], PX[:, hm * 2 + kt, b4 * 32:(b4 + 1) * 32], mvs[:, b4, kt, hm, :], kt == 0, kt == 1,
                                           [("PTx", gi % 2, hm * 2 + kt), ("mvs", b4), ("mvs1", 0)])
                                O = pf[Ob][0:32, 0:260].rearrange("p (h d) -> p h d", h=4)
                                dn = den[0:32, b4 % 2, :]
                                dk = ("den", b4 % 2)
                                OP("dve", "reciprocal", [], [("ps", Ob), dk], dn, O[:, :, 64])
                                OP("dve", "tensor_tensor", [dk], [("ps", Ob), ("atokx", 0, b4)], out=atok[0:32, 0, b4 * 256:(b4 + 1) * 256].rearrange("p (h d) -> p h d", h=4),
                                   in0=O[:, :, 0:64], in1=dn.unsqueeze(2).to_broadcast([32, 4, 64]), op=ALU.mult)

                    def xTR(gi):
                        n0, n1 = groups[gi]
                        tl = list(gt(n0, n1))
                        is_s = has_s and gi == 2
                        if not is_s:
                            for ti in tl:
                                par = ti % 2
                                kk = (ti // 2) % 4
                                bt = nbt()
                                for c in range(2):
                                    TRB(bt, pbt[bt][:, c * 128:(c + 1) * 128], atok[:, par, kk * 256 + c * 128:kk * 256 + (c + 1) * 128], identb[:], [("atokx", par, kk), "identb"])
                                OP("act", "activation", [], [("pt", bt)] + keys("co", [0, 1], [ti]), out=coT[:, :, ti * 128:(ti + 1) * 128], in_=pbt[bt][:, 0:256].rearrange("p (c t) -> p c t", c=2), func=AF.Copy)
                        else:
                            ti = s_ti
                            bt = nbt()
                            for b4 in range(4):
                                for c in range(2):
                                    TRB(bt, pbt[bt][:, c * 128 + b4 * 32:c * 128 + (b4 + 1) * 32], atok[0:32, 0, b4 * 256 + c * 128:b4 * 256 + (c + 1) * 128], identb[0:32, 0:32], [("atokx", 0, b4), "identb"])
                            OP("act", "activation", [], [("pt", bt)] + keys("co", [0, 1], [ti]), out=coT[:, :, ti * 128:(ti + 1) * 128], in_=pbt[bt][:, 0:256].rearrange("p (c t) -> p c t", c=2), func=AF.Copy)

                    def xWX(gi):
                        n0, n1 = groups[gi]
                        N = n1 - n0
                        tl = list(gt(n0, n1))
                        for j in range(8):
                            b = nb()
                            for kc in range(2):
                                MM(b, pf[b][:, 0:N], Wx[:, kc, j * 128:(j + 1) * 128], coT[:, kc, n0:n1], kc == 0, kc == 1, keys("co", [kc], tl) + [kx])
                            OP("dve", "tensor_tensor", keys("x", [j], tl), [("ps", b)] + keys("x", [j], tl), out=xT[:, j, n0:n1], in0=xT[:, j, n0:n1], in1=pf[b][:, 0:N], op=ALU.add)
                        rms(8 + l, [(n0, n1)])

                    if len(groups) == 3:
                        seq = [(xMIX, 0), (xMIX, 1), (xRMS, 0), (xMIX, 2), (xRMS, 1), (xCQ, 0), (xS, 0), (xRMS, 2), (xCQ, 1), (xS, 1), (xPREP, 0), (xPV, 0), (xCQ, 2), (xS, 2),
                               (xTR, 0), (xPV, 1), (xWX, 0), (xTR, 1), (xPV, 2), (xWX, 1), (xTR, 2), (xWX, 2)]
                    else:
                        seq = [(xMIX, 0), (xMIX, 1), (xRMS, 0), (xRMS, 1), (xCQ, 0), (xS, 0), (xCQ, 1), (xS, 1), (xPV, 0), (xTR, 0), (xPV, 1), (xWX, 0), (xTR, 1), (xWX, 1)]
                    for fn_, gi_x in seq:
                        fn_(gi_x)
                    stage(104)
                    P.barrier()

                    stage(11)
                    for hf in range(2):
                        tg = 0
                        for uu in range(4):
                            Wu, ku = next_unit("up")
                            for jj in range(4):
                                ci = uu * 4 + jj
                                for (n0, n1) in groups:
                                    N = n1 - n0
                                    tl = list(gt(n0, n1))
                                    b = nb()
                                    for kc in range(8):
                                        MM(b, pf[b][:, 0:N], Wu[:, kc, jj * 128:(jj + 1) * 128], hb[:, kc, n0:n1], kc == 0, kc == 7, keys("h", [kc], tl) + [ku])
                                    tg += 1
                                    tt, tk = (tB, "tB") if tg % 2 else (tC, "tC")
                                    OP("act", "activation", [], [("ps", b), tk], out=tt[:, 0:N], in_=pf[b][:, 0:N], func=AF.Relu)
                                    OP("dve", "tensor_tensor", [tk], keys("hid", [ci], tl), out=hid[:, ci, n0:n1], in0=tt[:, 0:N], in1=tt[:, 0:N], op=ALU.mult)
                        for j in range(8 if hf == 0 else 6):
                            Wd, kd = next_unit("down")
                            for (n0, n1) in groups:
                                N = n1 - n0
                                tl = list(gt(n0, n1))
                                b = nb()
                                for kc in range(16):
                                    MM(b, pf[b][:, 0:N], Wd[:, kc, :], hid[:, kc, n0:n1], kc == 0, kc == 15, keys("hid", [kc], tl) + [kd])
                                OP("dve", "tensor_tensor", keys("x", [j], tl), [("ps", b)] + keys("x", [j], tl), out=xT[:, j, n0:n1], in0=xT[:, j, n0:n1], in1=pf[b][:, 0:N], op=ALU.add)
                        if hf == 1:
                            Wd6, kd6 = next_unit("down")
                            Wd7, kd7 = next_unit("down", 2)
                            for (n0, n1) in groups:
                                N = n1 - n0
                                tl = list(gt(n0, n1))
                                for (Wd, kd, j) in ((Wd6, kd6, 6), (Wd7, kd7, 7)):
                                    b = nb()
                                    for kc in range(16):
                                        MM(b, pf[b][:, 0:N], Wd[:, kc, :], hid[:, kc, n0:n1], kc == 0, kc == 15, keys("hid", [kc], tl) + [kd])
                                    OP("dve", "tensor_tensor", keys("x", [j], tl), [("ps", b)] + keys("x", [j], tl), out=xT[:, j, n0:n1], in0=xT[:, j, n0:n1], in1=pf[b][:, 0:N], op=ALU.add)
                                if l < nlayers - 1:
                                    rms(l + 1, [(n0, n1)])
                    P.barrier()

                stage(12)
                for (n0, n1) in groups:
                    N = n1 - n0
                    tl = list(gt(n0, n1))
                    b = nb()
                    for c in range(8):
                        s2 = c % 2
                        OP("act", "activation", keys("x", [c], tl), [("sq", s2)], out=sq[:, s2, 0:N], in_=xT[:, c, n0:n1], func=AF.Square)
                        MM(b, pf[b][:, 0:N], onesb[:], sq[:, s2, 0:N], c == 0, c == 7, [("sq", s2), "onesb"])
                    OP("act", "activation", ["epsT"], [("ps", b), "tA"], out=tA[:, 0:N], in_=pf[b][:, 0:N], func=AF.Ln, scale=1.0 / 1024.0, bias=epsT[:])
                    OP("act", "activation", ["tA"], ["tA"], out=tA[:, 0:N], in_=tA[:, 0:N], func=AF.Exp, scale=-0.5)
                    for ti in tl:
                        kind, gi_ = tiles[ti]
                        off = ti * 128 - n0
                        dst = yp[gi_ * 128:(gi_ + 1) * 128, :] if kind == "p" else ys
                        for half in range(2):
                            tt, tk = (tB, "tB") if half == 0 else (tC, "tC")
                            hk = "xtok%d" % half
                            for i in range(4):
                                c = half * 4 + i
                                OP("dve", "scalar_tensor_tensor", keys("x", [c], [ti]) + ["tA", "gT"], [tk],
                                   out=tt[:, i * 128:(i + 1) * 128], in0=xT[:, c, ti * 128:(ti + 1) * 128], scalar=gT[:, c, 16:17], in1=tA[:, off:off + 128], op0=ALU.mult, op1=ALU.mult)
                            b2 = nb()
                            for i in range(4):
                                TRF(b2, pf[b2][:, i * 128:(i + 1) * 128], tt[:, i * 128:(i + 1) * 128], ident[:], [tk, "ident"])
                            OP("act", "activation", [], [("ps", b2), hk], out=xtok[:, half * 512:(half + 1) * 512], in_=pf[b2][:], func=AF.Copy)
                            DMA("sp", dst[:, half * 512:(half + 1) * 512], xtok[:, half * 512:(half + 1) * 512], [hk], [])
                P.barrier()

        try:
            _body()
        except _Stop:
            pass
        if _STOP == 0:
            assert wstate["used"] == len(units), (wstate["used"], len(units))
        P.wait_all_dma("sp")
        stats = P.emit(esems)
    return nc, stats


_CACHE = {}


def kernel(x_prompt, x_sample, mem_prompt, cache_attn_k, cache_attn_v, state_conv, cache_mem_k, cache_mem_v,
           norm_mix_g, w_in, conv_w, attn_sink, w_attn_out, w_conv_out, w_mix_out, norm_cross_g, norm_mem_g,
           w_cq, w_ckv, w_co, norm_mlp_g, w_up, w_down, norm_final_g):
    f = lambda a: np.ascontiguousarray(np.asarray(a, dtype=np.float32))
    x_prompt, x_sample, mem_prompt = f(x_prompt), f(x_sample), f(mem_prompt)
    cache_attn_k, cache_attn_v, state_conv = f(cache_attn_k), f(cache_attn_v), f(state_conv)
    cache_mem_k, cache_mem_v = f(cache_mem_k), f(cache_mem_v)
    w_in = f(w_in)
    qcols = np.concatenate([np.arange(h * 64, (h + 1) * 64) for h in QPERM_HEADS])
    perm = np.concatenate([qcols, np.arange(1024, 6656)])
    w_in_p = np.ascontiguousarray(w_in[:, :, perm])
    gall = np.ascontiguousarray(np.concatenate([f(norm_mix_g), f(norm_cross_g), f(norm_mlp_g), f(norm_mem_g),
                                                f(norm_final_g)[None, :], f(conv_w).reshape(12, 1024)], axis=0))
    if "nc" not in _CACHE:
        _CACHE["nc"] = build()[0]
    nc = _CACHE["nc"]
    shared = {
        "gall": gall, "sink": f(attn_sink), "w_in": w_in_p, "wao": f(w_attn_out), "wco": f(w_conv_out),
        "wmix": f(w_mix_out), "wcq": f(w_cq), "wckv": f(w_ckv), "wcox": f(w_co), "wup": f(w_up), "wdown": f(w_down),
    }
    in_maps = []
    for i in range(8):
        sl = slice(4 * i, 4 * i + 4)
        d = dict(shared)
        d["xp"] = x_prompt[i]
        d["xs"] = np.ascontiguousarray(x_sample[sl].reshape(128, 1024))
        d["mem"] = mem_prompt[i]
        d["cak"] = np.ascontiguousarray(cache_attn_k[:, sl].reshape(4, 4, 128, 256))
        d["cav"] = np.ascontiguousarray(cache_attn_v[:, sl].reshape(4, 4, 128, 256))
        d["sconv"] = np.ascontiguousarray(state_conv[:, sl].reshape(4, 8, 1024))
        d["cmk"] = np.ascontiguousarray(cache_mem_k[:, sl].reshape(4, 4, 256, 256))
        d["cmv"] = np.ascontiguousarray(cache_mem_v[:, sl].reshape(4, 4, 256, 256))
        in_maps.append(d)
    res = run_bass_kernel_spmd(nc, in_maps, core_ids=list(range(8)))
    R = res.results
    y_prompt = np.stack([R[i]["yp"] for i in range(8)], 0)
    y_sample = np.concatenate([R[i]["ys"].reshape(4, 32, 1024) for i in range(8)], 0)
    kpo = np.stack([R[i]["kp"].reshape(4, 128, 4, 64) for i in range(8)], 1)
    vpo = np.stack([R[i]["vp"].reshape(4, 128, 4, 64) for i in range(8)], 1)
    cpo = np.stack([R[i]["convp"] for i in range(8)], 1)
    mko = np.stack([R[i]["mkp"].reshape(4, 256, 4, 64) for i in range(8)], 1)
    mvo = np.stack([R[i]["mvp"].reshape(4, 256, 4, 64) for i in range(8)], 1)
    kso = np.concatenate([R[i]["ksn"].reshape(4, 4, 32, 4, 64) for i in range(8)], 1)
    vso = np.concatenate([R[i]["vsn"].reshape(4, 4, 32, 4, 64) for i in range(8)], 1)
    cso = np.concatenate([R[i]["convs"].reshape(4, 4, 2, 1024) for i in range(8)], 1)
    out = (y_prompt, y_sample, kpo, vpo, cpo, mko, mvo, kso, vso, cso)
    return tuple(np.ascontiguousarray(o.astype(np.float32)) for o in out)
```
